# Optimizing a Trainium2 kernel written in Bass

```python
import math
import jax, jax.numpy as jnp
from jax import lax
import numpy as np

D_MODEL = 1024
BATCH = 16
SEQ = 4096
DEPTH = 1

SB_WIDTH = D_MODEL // 2
SB_HEAD_DIM = 64
SB_HEADS = SB_WIDTH // SB_HEAD_DIM
ML_WIDTH = D_MODEL - SB_WIDTH
ML_HEADS = 4
ML_HEAD_DIM = ML_WIDTH // ML_HEADS
MIX_WIDTH = SB_WIDTH + ML_WIDTH
Q_BLOCK = 128
ML_CHUNK = 64
CONV_WIDTH = 4
EPS = 1e-6
IN_SPLITS = [SB_WIDTH] * 4 + [ML_WIDTH] * 5 + [ML_HEADS, ML_HEADS]
IN_WIDTH = sum(IN_SPLITS)

kernel_name = "hymba_stickbreaking_mlstm_adaln"


def _rmsnorm(t, gain):
    tf = t.astype(jnp.float32)
    tf = tf * lax.rsqrt(jnp.mean(tf * tf, axis=-1, keepdims=True) + EPS)
    return (tf * gain.astype(jnp.float32)).astype(t.dtype)


def _to_heads(t, n_heads):
    b, s, _ = t.shape
    return t.reshape(b, s, n_heads, -1).transpose(0, 2, 1, 3)


def _from_heads(t):
    b, h, s, d = t.shape
    return t.transpose(0, 2, 1, 3).reshape(b, s, h * d)


def _stick_breaking(q, k, v):
    seq = q.shape[2]
    scale = 1.0 / math.sqrt(q.shape[-1])
    outs = []
    for blk in range(seq // Q_BLOCK):
        t0 = blk * Q_BLOCK
        t1 = t0 + Q_BLOCK
        qb = q[:, :, t0:t1]
        kb = k[:, :, :t1]
        vb = v[:, :, :t1]
        z = jnp.einsum('bhqd,bhkd->bhqk', qb, kb).astype(jnp.float32) * scale
        qpos = t0 + jnp.arange(Q_BLOCK)[:, None]
        kpos = jnp.arange(t1)[None, :]
        strict = kpos < qpos
        log_1mb = jnp.where(strict, jax.nn.log_sigmoid(-z), 0.0)
        between = lax.cumsum(log_1mb, axis=3, reverse=True) - log_1mb
        a = jnp.where(strict, jnp.exp(jax.nn.log_sigmoid(z) + between), 0.0)
        outs.append(jnp.einsum('bhqk,bhkd->bhqd', a.astype(v.dtype), vb))
    return jnp.concatenate(outs, axis=2)


def _mlstm(q, k, v, i_pre, log_f):
    b, h, s, d = q.shape
    nc = s // ML_CHUNK
    f32 = jnp.float32
    qs = (q.astype(f32) * (1.0 / math.sqrt(d)))

    def chunks4(t):
        return t.reshape(b, h, nc, ML_CHUNK, -1).transpose(2, 0, 1, 3, 4)

    def chunks3(t):
        return t.reshape(b, h, nc, ML_CHUNK).transpose(2, 0, 1, 3)

    xs = (chunks4(qs), chunks4(k.astype(f32)), chunks4(v.astype(f32)),
          chunks3(i_pre.astype(f32)), chunks3(log_f.astype(f32)))
    causal = jnp.tril(jnp.ones((ML_CHUNK, ML_CHUNK), dtype=bool))

    def step(carry, xc):
        c_mat, n_vec, m_prev = carry
        qc, kc, vc, ic, fc = xc
        bcum = jnp.cumsum(fc, axis=-1)
        log_d = bcum[..., :, None] - bcum[..., None, :] + ic[..., None, :]
        log_d = jnp.where(causal, log_d, -jnp.inf)
        inter = bcum + m_prev[..., None]
        m_t = jnp.maximum(inter, jnp.max(log_d, axis=-1))
        d_mat = jnp.exp(log_d - m_t[..., None])
        g_inter = jnp.exp(inter - m_t)
        sc = jnp.einsum('bhld,bhsd->bhls', qc, kc) * d_mat
        num = (jnp.einsum('bhls,bhse->bhle', sc, vc)
               + g_inter[..., None] * jnp.einsum('bhld,bhde->bhle', qc, c_mat))
        den = jnp.sum(sc, axis=-1) + g_inter * jnp.einsum('bhld,bhd->bhl', qc, n_vec)
        h_out = num / jnp.maximum(jnp.abs(den), jnp.exp(-m_t))[..., None]
        b_last = bcum[..., -1]
        w_log = b_last[..., None] - bcum + ic
        m_new = jnp.maximum(b_last + m_prev, jnp.max(w_log, axis=-1))
        w = jnp.exp(w_log - m_new[..., None])
        g_c = jnp.exp(b_last + m_prev - m_new)
        c_new = g_c[..., None, None] * c_mat + jnp.einsum('bhs,bhsd,bhse->bhde', w, kc, vc)
        n_new = g_c[..., None] * n_vec + jnp.einsum('bhs,bhsd->bhd', w, kc)
        return (c_new, n_new, m_new), h_out

    init = (jnp.zeros((b, h, d, d), f32), jnp.zeros((b, h, d), f32), jnp.zeros((b, h), f32))
    _, hs = lax.scan(step, init, xs)
    return hs.transpose(1, 2, 0, 3, 4).reshape(b, h, s, d)


def _layer(x, c, w_ada, b_ada, norm_gain, w_in, b_gates, q_norm_gain, k_norm_gain,
           conv_w, conv_b, ml_norm_gain, w_out):
    bsz, seq, _ = x.shape
    mod = jax.nn.silu(c) @ w_ada + b_ada
    shift, scale, gate = jnp.split(mod, 3, axis=-1)
    hn = _rmsnorm(x, norm_gain) * (1.0 + scale[:, None, :]) + shift[:, None, :]
    u = hn @ w_in
    idx = list(np.cumsum(IN_SPLITS)[:-1])
    sb_q, sb_k, sb_v, sb_z, ml_q, ml_k, ml_v, ml_o, ml_z, ml_i, ml_f = jnp.split(u, idx, axis=-1)

    qh = _rmsnorm(_to_heads(sb_q, SB_HEADS), q_norm_gain)
    kh = _rmsnorm(_to_heads(sb_k, SB_HEADS), k_norm_gain)
    vh = _to_heads(sb_v, SB_HEADS)
    sb_out = _from_heads(_stick_breaking(qh, kh, vh))

    qk = jnp.concatenate([ml_q, ml_k], axis=-1)
    qk = lax.conv_general_dilated(
        qk, conv_w[:, None, :].astype(qk.dtype), window_strides=(1,),
        padding=[(CONV_WIDTH - 1, 0)], dimension_numbers=('NWC', 'WIO', 'NWC'),
        feature_group_count=2 * ML_WIDTH)
    qk = jax.nn.silu(qk + conv_b)
    mq, mk = jnp.split(qk, 2, axis=-1)
    gates = jnp.concatenate([ml_i, ml_f], axis=-1) + b_gates
    i_pre = gates[..., :ML_HEADS].transpose(0, 2, 1)
    log_f = jax.nn.log_sigmoid(gates[..., ML_HEADS:].astype(jnp.float32)).transpose(0, 2, 1)
    ml_h = _mlstm(_to_heads(mq, ML_HEADS), _to_heads(mk, ML_HEADS), _to_heads(ml_v, ML_HEADS),
                  i_pre, log_f)
    ml_h = _rmsnorm(ml_h, ml_norm_gain.reshape(ML_HEADS, 1, ML_HEAD_DIM)).astype(x.dtype)
    ml_out = jax.nn.sigmoid(ml_o) * _from_heads(ml_h)

    y = jnp.concatenate([sb_out * jax.nn.silu(sb_z), ml_out * jax.nn.silu(ml_z)], axis=-1)
    y = y @ w_out
    return x + gate[:, None, :] * y


def setup_inputs(seed: int = 0) -> dict:
    key = jax.random.key(seed)
    ks = jax.random.split(key, 16)
    f32 = jnp.float32
    nrm = lambda k, shp: jax.random.normal(k, shp, f32)
    x = nrm(ks[0], (BATCH, SEQ, D_MODEL))
    c = nrm(ks[1], (BATCH, D_MODEL))
    w_ada = nrm(ks[2], (DEPTH, D_MODEL, 3 * D_MODEL)) * (0.5 * D_MODEL ** -0.5)
    b_ada = nrm(ks[3], (DEPTH, 3 * D_MODEL)) * 0.02
    norm_gain = 1.0 + 0.02 * nrm(ks[4], (DEPTH, D_MODEL))
    w_in = nrm(ks[5], (DEPTH, D_MODEL, IN_WIDTH)) * D_MODEL ** -0.5
    i_bias = 0.02 * nrm(ks[6], (DEPTH, ML_HEADS))
    f_bias = jnp.linspace(3.0, 6.0, ML_HEADS, dtype=f32)[None, :] + 0.02 * nrm(ks[7], (DEPTH, ML_HEADS))
    b_gates = jnp.concatenate([i_bias, f_bias], axis=-1)
    q_norm_gain = 1.0 + 0.02 * nrm(ks[8], (DEPTH, SB_HEAD_DIM))
    k_norm_gain = 1.0 + 0.02 * nrm(ks[9], (DEPTH, SB_HEAD_DIM))
    conv_w = nrm(ks[10], (DEPTH, CONV_WIDTH, 2 * ML_WIDTH)) * CONV_WIDTH ** -0.5
    conv_b = 0.02 * nrm(ks[11], (DEPTH, 2 * ML_WIDTH))
    ml_norm_gain = 1.0 + 0.02 * nrm(ks[12], (DEPTH, ML_WIDTH))
    w_out = nrm(ks[13], (DEPTH, MIX_WIDTH, D_MODEL)) * MIX_WIDTH ** -0.5
    return {"x": x, "c": c, "w_ada": w_ada, "b_ada": b_ada, "norm_gain": norm_gain,
            "w_in": w_in, "b_gates": b_gates, "q_norm_gain": q_norm_gain,
            "k_norm_gain": k_norm_gain, "conv_w": conv_w, "conv_b": conv_b,
            "ml_norm_gain": ml_norm_gain, "w_out": w_out}


def reference(x, c, w_ada, b_ada, norm_gain, w_in, b_gates, q_norm_gain, k_norm_gain,
              conv_w, conv_b, ml_norm_gain, w_out):
    h = x
    for layer in range(DEPTH):
        h = _layer(h, c, w_ada[layer], b_ada[layer], norm_gain[layer], w_in[layer],
                   b_gates[layer], q_norm_gain[layer], k_norm_gain[layer], conv_w[layer],
                   conv_b[layer], ml_norm_gain[layer], w_out[layer])
    return h
```

```python
import math
from contextlib import ExitStack

import numpy as np
import concourse.bass as bass
import concourse.mybir as mybir
from concourse.bass_utils import run_bass_kernel_spmd

F32 = mybir.dt.float32
BF16 = mybir.dt.bfloat16
AF = mybir.ActivationFunctionType
ALU = mybir.AluOpType
AX = mybir.AxisListType

D = 1024
IN_W = 4616
EPS = 1e-6
N_CORES = 8


class Prog:
    ENG = ("pe", "act", "dve", "pool", "sp")

    def __init__(self, nc):
        self.nc = nc
        self.streams = {e: [] for e in self.ENG}
        self.cnt = {e: 0 for e in self.ENG}
        self.waited = {e: {} for e in self.ENG}
        self.writers = {}
        self.readers = {}
        self.dma_cnt = {}

    def op(self, eng, fn, reads=(), writes=(), dma=None):
        if any(isinstance(k, tuple) and len(k) == 2 and k[0] == "hnT" for k in reads):
            r2 = []
            for k in reads:
                if isinstance(k, tuple) and len(k) == 2 and k[0] == "hnT":
                    r2.extend(("hnT", k[1], kc) for kc in range(8))
                else:
                    r2.append(k)
            reads = tuple(r2)
        deps = {}

        def add(tok, same_ok):
            name, val, peng, isd = tok
            if (not isd) and peng == eng and not same_ok:
                return
            if deps.get(name, 0) < val:
                deps[name] = val

        for k in reads:
            for t in self.writers.get(k, {}).values():
                add(t, True)
        same_w = eng != "pe"
        for k in writes:
            sw = same_w and not (isinstance(k, tuple) and k[0] == "BANK")
            for t in self.writers.get(k, {}).values():
                add(t, sw)
            for t in self.readers.get(k, {}).values():
                add(t, sw)
        w = self.waited[eng]
        waits = []
        for name, val in deps.items():
            if w.get(name, 0) < val:
                waits.append((name, val))
                w[name] = val
        if dma is not None:
            self.dma_cnt[dma] = self.dma_cnt.get(dma, 0) + 16
            tok = (dma, self.dma_cnt[dma], eng, True)
        else:
            self.cnt[eng] += 1
            tok = ("E_" + eng, self.cnt[eng], eng, False)
        for k in writes:
            self.writers.setdefault(k, {})[tok[0]] = tok
            self.readers[k] = {}
        for k in reads:
            if k in writes:
                continue
            self.readers.setdefault(k, {})[tok[0]] = tok
        self.streams[eng].append((waits, fn, tok))

    def barrier(self):
        for e in self.ENG:
            waits = []
            w = self.waited[e]
            for f in self.ENG:
                if f != e and self.cnt[f] > w.get("E_" + f, 0):
                    waits.append(("E_" + f, self.cnt[f]))
                    w["E_" + f] = self.cnt[f]
            for name, val in self.dma_cnt.items():
                if val > w.get(name, 0):
                    waits.append((name, val))
                    w[name] = val
            if waits:
                self.streams[e].append((waits, None, None))

    def emit(self):
        nc = self.nc
        self.barrier()
        names = ["E_" + e for e in self.ENG] + sorted(self.dma_cnt.keys())
        with ExitStack() as es:
            sems = {n: es.enter_context(nc.semaphore(n)) for n in names}
            block = es.enter_context(nc.Block())

            def run(engname):
                def f(e):
                    for waits, fn, tok in self.streams[engname]:
                        for (n, v) in waits:
                            e.wait_ge(sems[n], v)
                        if fn is None:
                            continue
                        ins = fn(e)
                        ins.then_inc(sems[tok[0]], 16 if tok[3] else 1)
                return f

            block.tensor(run("pe"))
            block.scalar(run("act"))
            block.vector(run("dve"))
            block.gpsimd(run("pool"))
            block.sync(run("sp"))


def build(S=4096, NSEQ=2, do_attn=True, do_mlstm=True):
    nc = bass.Bass("TRN2", target_bir_lowering=False)
    P = Prog(nc)
    NT = S // 128
    NBK = S // 512
    HN = 4 * NT
    assert HN <= 128

    def dram(name, shape, kind="ExternalInput"):
        return nc.dram_tensor(name, list(shape), F32, kind=kind).ap()

    x_d = dram("x", [NSEQ, S, D])
    out_d = dram("out", [NSEQ, S, D], kind="ExternalOutput")
    cT_d = dram("cT", [128, 8, NSEQ])
    wada_d = dram("w_ada", [D, 3 * D])
    bada_fm_d = dram("b_ada_fm", [128, 24])
    bada_gate_d = dram("b_ada_gate", [1, D])
    ngain_d = dram("norm_gain_fm", [128, 8])
    win_d = dram("w_in", [D, IN_W])
    bg_d = dram("b_gates_bc", [128, 8])
    qg_d = dram("q_gain_fm", [128, 1])
    kg_d = dram("k_gain_fm", [128, 1])
    cw_d = dram("conv_w_fm", [128, 8, 4])
    cb_d = dram("conv_b_fm", [128, 8])
    mlg_d = dram("ml_gain_bc", [128, 512])
    wout_d = dram("w_out", [D, D])
    const_d = dram("consts", [128, 8, 128])

    def bk(*aps):
        keys = []
        for ap in aps:
            t = getattr(ap, "tensor", None)
            if t is None or getattr(t, "name", None) != "psb_all":
                continue
            pstride = t.shape[1]
            bank_elems = pstride // 8
            off = ap.offset % pstride
            span = 0
            for step, cnt in list(ap.ap)[1:]:
                span += (cnt - 1) * abs(step)
            for bnk in range(off // bank_elems, (off + span) // bank_elems + 1):
                k = ("BANK", bnk)
                if k not in keys:
                    keys.append(k)
        return tuple(keys)

    def mm(out, lhsT, rhs, start, stop, reads, writes, skip=False):
        P.op("pe", lambda e: e.matmul(out, lhsT, rhs, start=start, stop=stop,
                                      skip_group_check=skip), reads, tuple(writes) + bk(out))

    def tr(out, in_, ident, reads, writes):
        P.op("pe", lambda e: e.transpose(out, in_, ident), reads, tuple(writes) + bk(out))

    def act(out, in_, func, reads, writes, bias=None, scale=None):
        kw = {}
        if bias is not None:
            kw["bias"] = bias
        if scale is not None:
            kw["scale"] = scale
        P.op("act", lambda e: e.activation(out, in_, func, **kw), reads, tuple(writes) + bk(out, in_))

    def tt(eng, out, in0, in1, op, reads, writes):
        P.op(eng, lambda e: e.tensor_tensor(out, in0, in1, op), reads, tuple(writes) + bk(out, in0, in1))

    def ts(eng, out, in0, s1, s2, op0, op1, reads, writes):
        w2 = tuple(writes) + bk(out, in0, s1, s2)
        if s2 is None:
            P.op(eng, lambda e: e.tensor_scalar(out, in0, s1, None, op0), reads, w2)
        else:
            P.op(eng, lambda e: e.tensor_scalar(out, in0, s1, s2, op0, op1), reads, w2)

    def stt(out, in0, scalar, in1, op0, op1, reads, writes):
        P.op("dve", lambda e: e.scalar_tensor_tensor(out, in0, scalar, in1, op0, op1),
             reads, tuple(writes) + bk(out, in0, scalar, in1))

    def cp(eng, out, in_, reads, writes):
        w2 = tuple(writes) + bk(out, in_)
        if eng == "act":
            P.op("act", lambda e: e.copy(out, in_), reads, w2)
        else:
            P.op(eng, lambda e: e.tensor_copy(out, in_), reads, w2)

    def dveop(fn, aps, reads, writes):
        P.op("dve", fn, reads, tuple(writes) + bk(*aps))

    def memset(eng, ap, val, writes):
        P.op(eng, lambda e: e.memset(ap, val), (), writes)

    def dma(eng, out, in_, reads, writes, sem):
        P.op(eng, lambda e: e.dma_start(out=out, in_=in_), reads, writes, dma=sem)

    es = ExitStack()

    def sb(name, shape, dt=F32):
        return es.enter_context(nc.sbuf_tensor(name, list(shape), dt))

    hnT = sb("hnT", [128, 8, S], BF16)
    yT = sb("yT", [128, 8, S], BF16)
    NSTG = 3
    stg = [sb(f"stg{i}", [128, 1024], F32) for i in range(NSTG)]
    NWBF = 5
    wbf = [sb(f"wbf{i}", [128, 8, 128], BF16) for i in range(NWBF)]
    bufA = sb("bufA", [128, S], BF16)
    bufB = sb("bufB", [128, S], BF16)
    bufC = sb("bufC", [128, NT, 128], BF16)
    cf = sb("cf", [128, 4, 128], F32)
    cb16 = sb("cb16", [128, 8, 128], BF16)
    smalls = sb("smalls", [128, 64], F32)
    cw_sb = sb("cw_sb", [128, 8, 4], F32)
    mlg_sb = sb("mlg_sb", [128, 128], F32)

    C_ID, C_TM, C_MA, C_MM, C_BO, C_ON, C_PM = 0, 1, 2, 3, 4, 5, 6
    c_sc = slice(0, 8)
    c_mult = slice(8, 16)
    c_shift = slice(16, 24)
    c_bada = slice(24, 48)
    c_ng = slice(48, 56)
    c_gq = slice(56, 57)
    c_gk = slice(57, 58)
    c_bg = slice(58, 66 - 2)
    bg_sb = sb("bg_sb", [128, 8], F32)
    cbv_sb = sb("cbv_sb", [128, 8], F32)
    cT_sb = sb("cT_sb", [128, 8, NSEQ], F32)

    ARENA_F32 = 7280 if S >= 4096 else 14000
    arena = sb("arena", [128, ARENA_F32], F32)
    arena_off = [0]
    psb_all = es.enter_context(nc.psum_tensor("psb_all", [128, 4096], F32))
    psb = [psb_all[:, i * 512:(i + 1) * 512] for i in range(8)]
    psb_i = [0]

    def _shape_view(v, shape):
        if len(shape) == 2:
            return v
        if len(shape) == 3:
            return v.rearrange("p (a c) -> p a c", a=shape[1])
        raise ValueError(shape)

    class _View:
        def __init__(self, ap):
            self.ap = ap

        def __getitem__(self, k):
            return self.ap[k]

    def arena_reset():
        arena_off[0] = 0
        psb_i[0] = 0

    def lsb_arena(name, shape, dt=F32):
        esz = 4 if dt == F32 else 2
        n = 1
        for d in shape[1:]:
            n *= d
        nbytes = (n * esz + 31) // 32 * 32
        off = arena_off[0]
        assert off + nbytes <= ARENA_F32 * 4, (name, off, nbytes)
        arena_off[0] = off + nbytes
        v = arena[0:shape[0], off // 4:(off + nbytes) // 4]
        if dt != F32:
            v = v.bitcast(dt)
        v = v[:, 0:n]
        return _View(_shape_view(v, shape))

    def psum_arena(name, shape, dt=F32, bank=None):
        if bank is None:
            i = psb_i[0]
            psb_i[0] += 1
        else:
            i = bank
        assert i < 8
        v = psb[i]
        if dt != F32:
            v = v.bitcast(dt)
        n = 1
        for d in shape[1:]:
            n *= d
        v = v[0:shape[0], 0:n]
        r = _View(_shape_view(v, shape))
        r.bank = i
        return r

    stg_i = [0]

    def stg_next():
        i = stg_i[0] % NSTG
        stg_i[0] += 1
        return stg[i], ("stg", i), f"D_stg{i}"

    wbf_i = [0]

    def wbf_next():
        i = wbf_i[0] % NWBF
        wbf_i[0] += 1
        return wbf[i], ("wbf", i)

    cast_i = [0]

    def load_w_slice(col0, ncols=128):
        s, sk, ssem = stg_next()
        w, wk = wbf_next()
        sv = s[:, 0:8 * ncols].rearrange("p (k c) -> p k c", k=8)
        dma("sp", sv, win_d[:, col0:col0 + ncols].rearrange("(k p) c -> p k c", p=128),
            (), (sk,), ssem)
        eng = "pool" if cast_i[0] % 2 == 0 else "dve"
        cast_i[0] += 1
        cp(eng, w[:, :, 0:ncols], sv, (sk,), (wk,))
        return w, wk

    for ci, src in enumerate((0, 5, 3, 6)):
        dma("sp", cf[:, ci, :], const_d[:, src, :], (), (("cfp", ci),), f"D_cf{ci}")
    s0, s0k, s0sem = stg_next()
    dma("sp", s0[:], const_d.rearrange("p a c -> p (a c)"), (), (s0k,), s0sem)
    cp("dve", cb16[:].rearrange("p a c -> p (a c)"), s0[:], (s0k,), ("cb16",))
    cp("dve", smalls[:, 62:63], cf[:, 0, 0:1], tuple(("cfp", ci) for ci in range(4)), ("cf",))
    ts("dve", cb16[:, 5, :], cb16[:, C_TM, :], -1.0, None, ALU.mult, None, ("cb16",), ("cb16",))
    ts("dve", cb16[:, 7, :], cb16[:, C_MA, :], -1.0, None, ALU.mult, None, ("cb16",), ("cb16",))
    dma("sp", smalls[:, c_bada], bada_fm_d, (), ("sm_bada",), "D_sm1")
    dma("sp", smalls[:, c_ng], ngain_d, (), ("sm_ng",), "D_sm2")
    dma("sp", smalls[:, 60:61], qg_d, (), ("sm_gq0",), "D_sm3")
    dma("sp", smalls[:, 61:62], kg_d, (), ("sm_gk",), "D_sm4")
    dma("sp", bg_sb[:], bg_d, (), ("bg",), "D_bg")
    dma("sp", cbv_sb[:], cb_d, (), ("cbv",), "D_cbv")
    dma("sp", cw_sb[:], cw_d, (), ("cw",), "D_cw")
    dma("sp", cT_sb[:], cT_d, (), ("cT",), "D_cT")
    ts("dve", smalls[:, c_gq], smalls[:, 60:61], 0.125, None, ALU.mult, None, ("sm_gq0",), ("sm_gq",))
    gq_ap = smalls[:, c_gq]
    gk_ap = smalls[:, 61:62]

    ident_f = cf[:, 0, :]
    ones_f = cf[:, 1, :]
    tinc_f = cf[:, 2, :]
    pm_f = cf[:, 3, :]
    ident_b = cb16[:, C_ID, :]
    tm_b = cb16[:, C_TM, :]
    ma_b = cb16[:, C_MA, :]
    mmk_b = cb16[:, C_MM, :]
    bo_b = cb16[:, C_BO, :]
    ntm_b = cb16[:, 5, :]
    nma_b = cb16[:, 7, :]

    P.barrier()

    def phase_A(b, n_xslots=0):
        psum, lsb = psum_arena, lsb_arena

        ps_mod = psum(f"psmod{b}", [128, 512])
        ps_tl = [psum(f"pstl{b}_{i}", [128, 8, 128], BF16) for i in range(2)]
        ps_th = [psum(f"psth{b}_{i}", [128, 8, 128], BF16) for i in range(2)]
        junk2 = [lsb(f"junk{b}_{i}", [128, 1024], BF16) for i in range(1)] * 2
        xn = [lsb(f"xn{b}_{i}", [128, 1024], BF16) for i in range(2)]
        st = lsb(f"st{b}", [128, 4 * NT], F32)
        etmp = lsb(f"etmp{b}", [128, 8], F32)

        def a_setup():
            act(etmp[:], cT_sb[:, :, b], AF.Exp, ("cT",), ("etmp",), scale=-1.0)
            ts("dve", etmp[:], etmp[:], 1.0, None, ALU.add, None, ("etmp",), ("etmp",))
            P.op("dve", lambda e: e.reciprocal(etmp[:], etmp[:]), ("etmp",), ("etmp",))
            tt("dve", smalls[:, c_sc], etmp[:], cT_sb[:, :, b], ALU.mult, ("etmp", "cT"), ("sm_sc",))

            first = True
            for kc in range(8):
                for piece in range(2):
                    s, sk, ssem = stg_next()
                    dma("sp", s[:], wada_d[kc * 128:(kc + 1) * 128, piece * 1024:(piece + 1) * 1024],
                        (), (sk,), ssem)
                    for f8 in range(8):
                        ft = piece * 8 + f8
                        mm(ps_mod[:, ft:ft + 1], s[:, f8 * 128:(f8 + 1) * 128], smalls[:, kc:kc + 1],
                           first, (kc == 7 and ft == 15), (sk, "sm_sc"), ("psmod",), skip=True)
                        first = False
            tt("dve", smalls[:, c_shift], ps_mod[:, 0:8], smalls[:, 24:32], ALU.add,
               ("psmod", "sm_bada"), ("sm_shift",))
            tt("dve", smalls[:, c_mult], ps_mod[:, 8:16], smalls[:, 32:40], ALU.add,
               ("psmod", "sm_bada"), ("sm_mult",))
            stt(smalls[:, c_mult], smalls[:, c_mult], 1.0, smalls[:, c_ng], ALU.add, ALU.mult,
                ("sm_mult", "sm_ng"), ("sm_mult",))


        xs = {}
        xsl = [lsb(f"xsl{b}_{i}", [128, 1024], F32) for i in range(n_xslots)]
        LA = (n_xslots - 1) if n_xslots else 3

        def a_s1a(t):
            if n_xslots:
                i = t % n_xslots
                s, sk, ssem = xsl[i], ("xsl", i), f"D_xsl{i}"
            else:
                s, sk, ssem = stg_next()
            xs[t] = (s, sk)
            dma("sp", s[:], x_d[b, t * 128:(t + 1) * 128, :], (), (sk,), ssem)
            junk = junk2[t % 2]
            act(junk[:], s[:], AF.Square, (sk,), ("junk",))
            P.op("dve", lambda e, o=st[:, 4 * t:4 * t + 1], junk=junk: e.reduce_sum(o, junk[:], AX.X),
                 ("junk",), (("st", t),))

        def a_s1b(t):
            s, sk = xs[t]
            act(st[:, 4 * t + 1:4 * t + 2], st[:, 4 * t:4 * t + 1], AF.Ln, (("st", t),), (("st1", t),),
                bias=EPS, scale=1.0 / D)
            act(st[:, 4 * t + 2:4 * t + 3], st[:, 4 * t + 1:4 * t + 2], AF.Exp, (("st1", t),),
                (("st2", t),), scale=-0.5)
            ts("dve", xn[t % 2][:], s[:], st[:, 4 * t + 2:4 * t + 3], None, ALU.mult, None,
               (sk, ("st2", t)), (("xn", t % 2),))

        def a_s2(t):
            xb = xn[t % 2]
            for kc in range(8):
                pt = (ps_tl if kc < 4 else ps_th)[t % 2]
                tr(pt[:, kc % 4, :], xb[:, kc * 128:(kc + 1) * 128], ident_b,
                   (("xn", t % 2), "cb16"), (("pst", t % 2, kc // 4),))
            for kc in range(8):
                o = hnT[:, kc, t * 128:(t + 1) * 128]
                if kc < 4:
                    ts("dve", o, ps_tl[t % 2][:, kc, :], smalls[:, 8 + kc:9 + kc], smalls[:, 16 + kc:17 + kc],
                       ALU.mult, ALU.add, (("pst", t % 2, 0), "sm_mult", "sm_shift"), (("hnT", t // 4, kc),))
                else:
                    act(o, ps_th[t % 2][:, kc - 4, :], AF.Identity, (("pst", t % 2, 1), "sm_mult", "sm_shift"),
                        (("hnT", t // 4, kc),), bias=smalls[:, 16 + kc:17 + kc],
                        scale=smalls[:, 8 + kc:9 + kc])

        def a_iter(t):
            if 0 <= t + LA < NT:
                a_s1a(t + LA)
            if 0 <= t + 1 < NT:
                a_s1b(t + 1)
            if 0 <= t < NT:
                a_s2(t)
        return a_iter, a_setup

    def phase_D(b, nd=3, store_eng="act"):
        psum, lsb = psum_arena, lsb_arena

        ps_x = [psum(f"dpsx{b}_{i}", [128, 512]) for i in range(2)]
        ps_gt = ps_x
        tmpD = lsb(f"tmpD{b}", [128, 1024], F32)
        scbc = tmpD[:].rearrange("p (k c) -> p k c", k=8)
        gate = lsb(f"gate{b}", [128, D], F32)
        if S >= 4096:
            def wo(kc):
                return (bufA if kc < 4 else bufB)[:, (kc % 4) * 1024:(kc % 4 + 1) * 1024]
        else:
            wo_ar = lsb(f"wobf{b}", [128, 8, D], BF16)

            def wo(kc):
                return wo_ar[:, kc, :]
        dsl = [lsb(f"dsl{b}_{i}", [128, 1024], F32) for i in range(nd)]
        otmp = [tmpD[:, 0:512], tmpD[:, 512:1024]]

        cp("dve", scbc, smalls[:, c_sc].unsqueeze(2).broadcast_to([128, 8, 128]), ("sm_sc",), ("scbc",))
        sbr, sbrk, sbrsem = stg_next()
        dma("sp", sbr[0:1, :], bada_gate_d, (), (sbrk,), sbrsem)
        for hf in range(2):
            mm(ps_gt[hf][:], ones_f[0:1, :], sbr[0:1, hf * 512:(hf + 1) * 512], True, False,
               (sbrk, "cf"), (("psgt", hf),))
        for kc in range(8):
            s, sk, ssem = stg_next()
            dma("sp", s[:], wada_d[kc * 128:(kc + 1) * 128, 2048:3072], (), (sk,), ssem)
            for hf in range(2):
                mm(ps_gt[hf][:], scbc[:, kc, :], s[:, hf * 512:(hf + 1) * 512], False, kc == 7,
                   ("scbc", sk), (("psgt", hf),))
        for hf in range(2):
            cp("dve", gate[:, hf * 512:(hf + 1) * 512], ps_gt[hf][:], (("psgt", hf),), ("gate",))
        for kc in range(8):
            s, sk, ssem = stg_next()
            dma("sp", s[:], wout_d[kc * 128:(kc + 1) * 128, :], (), (sk,), ssem)
            cp("pool" if kc % 2 == 0 else "dve", wo(kc), s[:], (sk,), ("wobf",))
        ykeys = lambda t: tuple(("yT", p, t // 4) for p in range(8)) + tuple(("yT", p, 0) for p in range(8))
        pend = []

        def d_iter(t):
            i = t % nd
            s, sk, ssem = dsl[i], ("dsl", i), f"D_dsl{i}"
            dma("sp", s[:], x_d[b, t * 128:(t + 1) * 128, :], (), (sk,), ssem)
            for hf in range(2):
                px = ps_x[hf]
                for kc in range(8):
                    mm(px[:], yT[:, kc, t * 128:(t + 1) * 128], wo(kc)[:, hf * 512:(hf + 1) * 512],
                       kc == 0, kc == 7, ykeys(t) + ("wobf",), (("psx", hf),))
                tt("dve", otmp[hf], px[:], gate[:, hf * 512:(hf + 1) * 512], ALU.mult,
                   (("psx", hf), "gate"), (("otmp", hf), "scbc"))
                tt("pool", s[:, hf * 512:(hf + 1) * 512], s[:, hf * 512:(hf + 1) * 512], otmp[hf], ALU.add,
                   (sk, ("otmp", hf)), (sk,))
            if pend:
                pend.pop(0)()
            pend.append(lambda s=s, sk=sk, i=i, t=t:
                        dma(store_eng, out_d[b, t * 128:(t + 1) * 128, :], s[:], (sk,), (), f"D_dst{i}"))

        def d_flush():
            while pend:
                pend.pop(0)()
        return d_iter, d_flush

    arena_reset()
    a_it, a_setup = phase_A(0, n_xslots=5)
    for t in range(-4, 0):
        a_it(t)
    a_setup()
    for t in range(0, NT):
        a_it(t)
    P.barrier()

    for b in range(NSEQ):

        if do_attn:
            with ExitStack() as ph:
                arena_reset()
                lsb = lsb_arena
                zz = psb_all[:, 0:1024].rearrange("p (h c) -> p h c", h=2)
                gg = psb_all[:, 2048:3072].rearrange("p (h c) -> p h c", h=2)
                ps_x = [psb[2], psb[3], psb[0]]
                ps_gp = [psb[4], psb[5]]
                ps_o = [psb[6], psb[7]]
                Pb = [lsb(f"Pb{b}_{i}", [128, 2, 512], F32) for i in range(2)]
                Lb = [lsb(f"Lb{b}_{i}", [128, 2, 512], BF16) for i in range(3)]
                KTn = [lsb(f"KTn{b}_{i}", [128, 128], BF16) for i in range(3)]
                Ab = [lsb(f"Ab{b}_{i}", [128, 2, 512], BF16) for i in range(2)]
                SZ = [lsb(f"SZ{b}_{i}", [128, 512], BF16) for i in range(2)]
                ztmp = lsb(f"ztmp{b}", [128, 512], F32)
                QTz = [lsb(f"QTz{b}_{i}", [128, 2, 512], BF16) for i in range(2)]
                QT, KT, V = bufA, bufB, bufC
                for sl in range(2):
                    memset("pool", QTz[sl][:], 0.0, (("QTz", sl),))

                def load_pair(p):
                    return (load_w_slice(0 + 128 * p), load_w_slice(512 + 128 * p),
                            load_w_slice(1024 + 128 * p), load_w_slice(1536 + 128 * p))

                nxt_w = load_pair(0)
                for p in range(4):
                    (wq, wqk), (wk_, wkk), (wv, wvk), (wz, wzk) = nxt_w

                    jobs = []
                    for (wsl, wkey, dst, dkey, gain, gkey) in ((wq, wqk, QT, "QT", gq_ap, "sm_gq"),
                                                              (wk_, wkk, KT, "KT", gk_ap, "sm_gk")):
                        for blk in range(NBK):
                            jobs.append((wsl, wkey, dst, dkey, gain, gkey, blk))
                    NJ = len(jobs)

                    def pj_mm(n):
                        wsl, wkey, dst, dkey, gain, gkey, blk = jobs[n]
                        px = ps_x[n % 3]
                        cols = slice(blk * 512, (blk + 1) * 512)
                        for kc in range(8):
                            mm(px, wsl[:, kc, :], hnT[:, kc, cols], kc == 0, kc == 7,
                               (wkey, ("hnT", blk)), (("psx", n % 3),))

                    def pj_sq(n):
                        act(Lb[n % 2][:, 0, :], ps_x[n % 3], AF.Square, (("psx", n % 3),), (("Lb", n % 2),))
                        mm(ps_gp[n % 2], bo_b, Lb[n % 2][:, 0, :], True, True, (("Lb", n % 2), "cb16"),
                           (("gg", n % 2),))

                    def pj_fin(n):
                        wsl, wkey, dst, dkey, gain, gkey, blk = jobs[n]
                        cols = slice(blk * 512, (blk + 1) * 512)
                        r = Pb[n % 2][:, 0, :]
                        act(r, ps_gp[n % 2], AF.Ln, (("gg", n % 2),), (("Pb", n % 2),), bias=EPS, scale=1.0 / 64)
                        act(r, r, AF.Exp, (("Pb", n % 2),), (("Pb", n % 2),), scale=-0.5)
                        stt(dst[:, cols], ps_x[n % 3], gain, r, ALU.mult, ALU.mult,
                            (("psx", n % 3), ("Pb", n % 2), gkey), ((dkey, blk),))

                    for n in range(-2, NJ):
                        if 0 <= n + 2 < NJ:
                            pj_mm(n + 2)
                        if 0 <= n + 1 < NJ:
                            pj_sq(n + 1)
                        if 0 <= n:
                            pj_fin(n)

                    for t0 in range(0, NT, 4):
                        pi = (t0 // 4) % 2
                        px = ps_x[pi]
                        pxk = ("psx", pi)
                        for ti in range(4):
                            t = t0 + ti
                            for kc in range(8):
                                mm(px[:, ti * 128:(ti + 1) * 128], hnT[:, kc, t * 128:(t + 1) * 128],
                                   wv[:, kc, :], kc == 0, kc == 7, (wvk, ("hnT", t // 4)), (pxk,), skip=True)
                        cp("dve", V[:, t0:t0 + 4, :], px.rearrange("p (a c) -> p a c", a=4),
                           (pxk,), (("V", t0 // 4),))

                    if p + 1 < 4:
                        nxt_w = load_pair(p + 1)
                    steps = []
                    for qb in range(NBK):
                        nk = 4 * (qb + 1)
                        for kt in range(nk - 1, -1, -1):
                            r = kt - 4 * qb
                            c0 = 128 * r if r >= 0 else 0
                            steps.append(dict(qb=qb, kt=kt, c0=c0, diag=(r >= 0),
                                              first=(kt == nk - 1), last=(kt == 0)))
                    NS = len(steps)

                    def prep(qb):
                        cols = slice(qb * 512, (qb + 1) * 512)
                        sl = qb % 2
                        px = ps_x[sl]
                        pxk = ("psx", sl)
                        for kc in range(8):
                            mm(px, wz[:, kc, :], hnT[:, kc, cols], kc == 0, kc == 7,
                               (wzk, ("hnT", qb)), (pxk,))
                        act(ztmp[:], px, AF.Exp, (pxk,), ("ztmp",), scale=-1.0)
                        ts("dve", ztmp[:], ztmp[:], 1.0, None, ALU.add, None, ("ztmp",), ("ztmp",))
                        P.op("dve", lambda e: e.reciprocal(ztmp[:], ztmp[:]), ("ztmp",), ("ztmp",))
                        tt("dve", SZ[sl][:], px, ztmp[:], ALU.mult, (pxk, "ztmp"), (("SZ", sl),))
                        for hd in range(2):
                            pr = slice(64 * hd, 64 * hd + 64)
                            cp("pool", QTz[sl][pr, hd, :], QT[pr, cols], (("QT", qb),), (("QTz", sl),))

                    def S1(j):
                        T = steps[j]
                        if T["first"]:
                            prep(T["qb"])
                        c0, qb, kt = T["c0"], T["qb"], T["kt"]
                        for hd in range(2):
                            mm(zz[:, hd, c0:512], KT[:, kt * 128:(kt + 1) * 128], QTz[qb % 2][:, hd, c0:512],
                               True, True, (("KT", kt // 4), ("QTz", qb % 2)), ("zz", ("psx", 2)))

                    def S2p(j):
                        c0 = steps[j]["c0"]
                        act(Pb[j % 2][:, :, c0:512], zz[:, :, c0:512], AF.Exp, ("zz", ("psx", 2)), (("Pb", j % 2),))

                    def S2l(j):
                        T = steps[j]
                        c0, kt = T["c0"], T["kt"]
                        act(Lb[j % 3][:, :, c0:512], Pb[j % 2][:, :, c0:512], AF.Ln,
                            (("Pb", j % 2),), (("Lb", j % 3),), bias=1.0)
                        if T["diag"]:
                            tt("dve", Lb[j % 3][:, :, c0:c0 + 128], Lb[j % 3][:, :, c0:c0 + 128],
                               ma_b.unsqueeze(1).broadcast_to([128, 2, 128]), ALU.mult,
                               (("Lb", j % 3), "cb16"), (("Lb", j % 3),))
                        if not T["last"]:
                            ts("pool", KTn[j % 3][:], KT[:, kt * 128:(kt + 1) * 128], -1.0, None, ALU.mult, None,
                               (("KT", kt // 4),), (("KTn", j % 3),))

                    def S3(j, hd):
                        T = steps[j]
                        c0, qb, kt = T["c0"], T["qb"], T["kt"]
                        mm(gg[:, hd, c0:512], ntm_b, Lb[j % 3][:, hd, c0:512], T["first"], False,
                           (("Lb", j % 3), "cb16"), (("gg", hd),), skip=True)
                        mm(gg[:, hd, c0:512], KT[:, kt * 128:(kt + 1) * 128], QTz[qb % 2][:, hd, c0:512],
                           False, True, (("KT", kt // 4), ("QTz", qb % 2)), (("gg", hd),), skip=True)

                    def S4(j, hd):
                        T = steps[j]
                        c0 = T["c0"]
                        act(Ab[j % 2][:, hd, c0:512], gg[:, hd, c0:512], AF.Exp,
                            (("gg", hd),), (("Ab", j % 2, hd),))
                        if T["diag"]:
                            tt("dve", Ab[j % 2][:, hd, c0:c0 + 128], Ab[j % 2][:, hd, c0:c0 + 128],
                               ma_b, ALU.mult, (("Ab", j % 2, hd), "cb16"), (("Ab", j % 2, hd),))

                    def S5fix(j, hd):
                        T = steps[j]
                        c0, qb, kt = T["c0"], T["qb"], T["kt"]
                        if T["last"]:
                            return
                        mm(gg[:, hd, c0:512], KTn[j % 3][:], QTz[qb % 2][:, hd, c0:512], False, False,
                           (("KTn", j % 3), ("QTz", qb % 2)), (("gg", hd),), skip=True)
                        mm(gg[:, hd, c0:512], nma_b, Lb[j % 3][:, hd, c0:512], False, True,
                           (("Lb", j % 3), "cb16"), (("gg", hd),), skip=True)

                    def S5o(j):
                        T = steps[j]
                        c0, qb, kt = T["c0"], T["qb"], T["kt"]
                        for hd in range(2):
                            o = ps_o[hd]
                            mm(o[:, c0:512], V[:, kt, :], Ab[j % 2][:, hd, c0:512],
                               T["first"], T["last"], (("V", kt // 4), ("Ab", j % 2, hd)), (("pso", hd),), skip=True)
                            if T["last"]:
                                pb = 64 * hd
                                cols = slice(qb * 512, (qb + 1) * 512)
                                tt("dve", yT[pb:pb + 64, p, cols], o[pb:pb + 64, :], SZ[qb % 2][pb:pb + 64, :],
                                   ALU.mult, (("pso", hd), ("SZ", qb % 2)), (("yT", p, qb),))

                    for j in range(-4, NS):
                        for hd in range(2):
                            if 0 <= j:
                                S4(j, hd)
                                S5fix(j, hd)
                            if 0 <= j + 1 < NS:
                                S3(j + 1, hd)
                        if 0 <= j + 2 < NS:
                            S2l(j + 2)
                        if 0 <= j + 3 < NS:
                            S2p(j + 3)
                        if 0 <= j:
                            S5o(j)
                        if 0 <= j + 4 < NS:
                            S1(j + 4)
            P.barrier()
        else:
            for p in range(4):
                memset("pool", yT[:, p, :], 0.0, (("yT", p, 0),))

        if do_mlstm:
            with ExitStack() as ph:
                arena_reset()
                psum, lsb = psum_arena, lsb_arena

                ps_x = [psum(f"cpsx{b}_{i}", [128, 512]) for i in range(2)]
                ps_t = psum(f"cpst{b}", [128, 8, 128], BF16)
                ps_s = psum(f"cpss{b}", [128, 512])
                ps_n = psum(f"cpsn{b}", [128, 512])
                ps_n2 = [ps_n, psum(f"cpsn2{b}", [128, 512])]
                ps_c = psum(f"cpsc{b}", [128, 512])
                ps_m = psum(f"cpsm{b}", [128, 512])
                ps_s2 = [ps_s, psum(f"cpss2{b}", [128, 512], bank=ps_x[0].bank)]
                ps_c2 = [ps_c, psum(f"cpsc2{b}", [128, 512], bank=ps_x[1].bank)]
                ps_t2 = [ps_t, psum(f"cpst2{b}", [128, 8, 128], BF16, bank=ps_m.bank)]

                gi = lsb(f"gi{b}", [128, HN])
                gf = lsb(f"gf{b}", [128, HN])
                t1f = lsb(f"t1{b}", [128, 128])
                t1 = t1f[:, 0:HN]
                Bn = lsb(f"Bn{b}", [128, HN])
                lfn = gf
                Xs = t1f
                ag = gi
                cmx = lsb(f"cmx{b}", [128, 1])
                rows = lsb(f"rows{b}", [1, 3, HN])
                colfac = lsb(f"colfac{b}", [128, HN])
                thr = lsb(f"thr{b}", [128, HN])
                gcb = lsb(f"gcb{b}", [128, HN])
                Vp = lsb(f"Vp{b}", [128, NT, 129], BF16)
                hbuf = lsb(f"hbuf{b}", [128, NT * 128 + 8], BF16)
                Dg = lsb(f"Dg{b}", [128, 8, 128], BF16)
                S0m = [lsb(f"S0m{b}_{i}", [128, 128], BF16) for i in range(2)]
                Ktok = [lsb(f"Ktok{b}_{i}", [128, 128], BF16) for i in range(2)]
                U = lsb(f"U{b}", [128, 129], F32)
                Cbf = [lsb(f"Cbf{b}_{i}", [128, 129], BF16) for i in range(2)]
                den = lsb(f"den{b}", [128, 2 * NT], F32)
                ssq = lsb(f"ssq{b}", [128, 3 * NT], F32)
                tmpa = lsb(f"tmpa{b}", [128, 512], BF16)
                qT, kT, W = bufA, bufB, bufC

                s, sk, ssem = stg_next()
                wg, wgk = wbf_next()
                sv = s[:, 0:64].rearrange("p (k c) -> p k c", k=8)
                dma("sp", sv, win_d[:, 4608:4616].rearrange("(k p) c -> p k c", p=128), (), (sk,), ssem)
                cp("dve", wg[:, :, 0:8], sv, (sk,), (wgk,))
                for t in range(NT):
                    for kc in range(8):
                        mm(ps_m[:, t * 8:(t + 1) * 8], hnT[:, kc, t * 128:(t + 1) * 128], wg[:, kc, 0:8],
                           kc == 0, kc == 7, (wgk, ("hnT", t // 4)), ("psm",), skip=True)
                pm3 = ps_m[:, 0:NT * 8].rearrange("p (j c) -> p j c", c=8)
                gi3 = gi[:].rearrange("p (h j) -> p h j", h=4)
                gf3 = gf[:].rearrange("p (h j) -> p h j", h=4)
                tt("dve", gi3, pm3[:, :, 0:4].rearrange("p j h -> p h j"),
                   bg_sb[:, 0:4].unsqueeze(2).broadcast_to([128, 4, NT]), ALU.add, ("psm", "bg"), ("gi",))
                tt("dve", gf3, pm3[:, :, 4:8].rearrange("p j h -> p h j"),
                   bg_sb[:, 4:8].unsqueeze(2).broadcast_to([128, 4, NT]), ALU.add, ("psm", "bg"), ("gf",))
                act(t1[:], gf[:], AF.Exp, ("gf",), ("t1",), scale=-1.0)
                act(lfn[:], t1[:], AF.Ln, ("t1",), ("gf",), bias=1.0)
                mm(ps_s[0:HN, 0:128], lfn[:], ones_f, True, True, ("gf", "cf"), (("pss", 0), ("pss", 1),))
                cp("dve", Xs[0:HN, :], ps_s[0:HN, 0:128], (("pss", 0), ("pss", 1),), ("t1",))
                mm(ps_n[:, 0:HN], tinc_f, lfn[:], True, False, ("gf", "cf"), (("psn", 0),))
                mm(ps_n[:, 0:HN], Xs[0:HN, :], pm_f[0:HN, 0:HN], False, True, ("t1", "cf"), (("psn", 0),))
                cp("dve", Bn[:], ps_n[:, 0:HN], (("psn", 0),), ("Bn",))
                tt("dve", ag[:], gi[:], Bn[:], ALU.add, ("gi", "Bn"), ("gi",))
                tr(ps_s[0:HN, 0:128], ag[:], ident_f, ("gi", "cf"), (("pss", 0), ("pss", 1),))
                P.op("dve", lambda e: e.reduce_max(cmx[0:HN, :], ps_s[0:HN, 0:128], AX.X), (("pss", 0), ("pss", 1),),
                     ("cmx",) + bk(ps_s[0:HN, 0:128]))
                mm(ps_c[0:1, 0:HN], cmx[0:HN, :], ident_f[0:HN, 0:HN], True, True, ("cmx", "cf"), (("psc", 0), ("psc", 1),))
                cp("dve", rows[:, 0, :], ps_c[0:1, 0:HN], (("psc", 0), ("psc", 1),), ("rows0",))
                for h in range(4):
                    seg = slice(h * NT, (h + 1) * NT)
                    P.op("dve", lambda e, seg=seg: e.tensor_tensor_scan(
                        rows[:, 1, seg], rows[:, 0, seg], rows[:, 0, seg], 0.0, ALU.max, ALU.max),
                        ("rows0",), ("rows1",))
                memset("dve", rows[:, 2, :], 0.0, ("rows2",))
                if NT > 1:
                    r1 = rows[:, 1, :].rearrange("p (h j) -> p h j", h=4)
                    r2 = rows[:, 2, :].rearrange("p (h j) -> p h j", h=4)
                    cp("dve", r2[:, :, 1:NT], r1[:, :, 0:NT - 1], ("rows1", "rows2"), ("rows2",))
                mm(ps_s[:, 0:HN], ones_f[0:1, :], rows[:, 2, :], True, True, ("rows2", "cf"), (("pss", 0), ("pss", 1),))
                mm(ps_n[:, 0:HN], ones_f[0:1, :], rows[:, 1, :], True, True, ("rows1", "cf"), (("psn", 0),))
                tt("dve", t1[:], ag[:], ps_s[:, 0:HN], ALU.subtract, ("gi", ("pss", 0), ("pss", 1)), ("t1",))
                act(colfac[:], t1[:], AF.Exp, ("t1",), ("colfac",))
                tt("dve", t1[:], Bn[:], ps_s[:, 0:HN], ALU.subtract, ("Bn", ("pss", 0), ("pss", 1)), ("t1",))
                act(thr[:], t1[:], AF.Exp, ("t1",), ("thr0",))
                ts("dve", thr[:], thr[:], math.sqrt(128.0), None, ALU.mult, None, ("thr0",), ("thr",))
                cp("dve", t1[:], ps_s[:, 0:HN], (("pss", 0), ("pss", 1),), ("t1",))
                tt("dve", t1[:], t1[:], ps_n[:, 0:HN], ALU.subtract, ("t1", ("psn", 0)), ("t1",))
                act(gcb[:], t1[:], AF.Exp, ("t1",), ("gcb",))

                def load_head(h):
                    return (load_w_slice(2048 + 128 * h), load_w_slice(2560 + 128 * h),
                            load_w_slice(3072 + 128 * h), load_w_slice(3584 + 128 * h),
                            load_w_slice(4096 + 128 * h))

                nxt_hw = load_head(0)
                for h in range(4):
                    (wq, wqk), (wk_, wkk), (wv, wvk), (wo, wok), (wz, wzk) = nxt_hw
                    dma("sp", mlg_sb[:], mlg_d[:, h * 128:(h + 1) * 128], (), ("mlg",), "D_mlg")
                    for j in range(4):
                        ts("dve", Dg[:, j, :], ident_f, cw_sb[:, h, j:j + 1], None, ALU.mult, None,
                           ("cf", "cw"), ("Dg",))
                        ts("dve", Dg[:, 4 + j, :], ident_f, cw_sb[:, 4 + h, j:j + 1], None, ALU.mult, None,
                           ("cf", "cw"), ("Dg",))
                    memset("pool", hbuf[:, 0:3], 0.0, ("hbuf",))
                    for (wsl, wkey, dst, dkey, dgo, cbi) in ((wq, wqk, qT, "QT", 0, h),
                                                             (wk_, wkk, kT, "KT", 4, 4 + h)):
                        for blk in range(NBK):
                            px = ps_x[blk % 2]
                            pxk = ("psx", blk % 2)
                            cols = slice(blk * 512, (blk + 1) * 512)
                            for kc in range(8):
                                mm(px[:], wsl[:, kc, :], hnT[:, kc, cols], kc == 0, kc == 7,
                                   (wkey, ("hnT", blk)), (pxk,))
                            cp("dve", hbuf[:, 3 + blk * 512:3 + (blk + 1) * 512], px[:], (pxk,), ("hbuf",))
                        for blk in range(NBK):
                            px = ps_x[blk % 2]
                            pxk = ("psx", blk % 2)
                            cols = slice(blk * 512, (blk + 1) * 512)
                            for j in range(4):
                                mm(px[:], Dg[:, dgo + j, :], hbuf[:, blk * 512 + j:blk * 512 + j + 512],
                                   j == 0, j == 3, ("Dg", "hbuf"), (pxk,))
                            act(dst[:, cols], px[:], AF.Silu, (pxk, "cbv"), ((dkey, blk),),
                                bias=cbv_sb[:, cbi:cbi + 1])
                    for t0 in range(0, NT, 4):
                        px = ps_x[(t0 // 4) % 2]
                        pxk = ("psx", (t0 // 4) % 2)
                        for ti in range(4):
                            t = t0 + ti
                            for kc in range(8):
                                mm(px[:, ti * 128:(ti + 1) * 128], hnT[:, kc, t * 128:(t + 1) * 128],
                                   wv[:, kc, :], kc == 0, kc == 7, (wvk, ("hnT", t // 4)), (pxk,), skip=True)
                        tt("dve", Vp[:, t0:t0 + 4, 0:128], px[:].rearrange("p (a c) -> p a c", a=4),
                           colfac[:, h * NT + t0:h * NT + t0 + 4].unsqueeze(2).broadcast_to([128, 4, 128]),
                           ALU.mult, (pxk, "colfac"), ("Vp",))
                    cp("dve", Vp[:, :, 128:129], colfac[:, h * NT:(h + 1) * NT].unsqueeze(2),
                       ("colfac",), ("Vp",))
                    for t0 in range(0, NT, 4):
                        for (wsl, wkey, fn, dstt, pi) in ((wo, wok, AF.Sigmoid, tmpa, 0), (wz, wzk, AF.Silu, hbuf, 1)):
                            px = ps_x[pi]
                            pxk = ("psx", pi)
                            for ti in range(4):
                                t = t0 + ti
                                for kc in range(8):
                                    mm(px[:, ti * 128:(ti + 1) * 128], hnT[:, kc, t * 128:(t + 1) * 128],
                                       wsl[:, kc, :], kc == 0, kc == 7, (wkey, ("hnT", t // 4)), (pxk,), skip=True)
                            act(dstt[:, 0:512], px[:], fn, (pxk,), (("tmp", 0) if pi == 0 else "hbuf",))
                        tt("dve", tmpa[:], tmpa[:], hbuf[:, 0:512], ALU.mult, (("tmp", 0), "hbuf"), (("tmp", 0),))
                        tt("pool", W[:, t0:t0 + 4, :], tmpa[:].rearrange("p (a c) -> p a c", a=4),
                           mlg_sb[:].unsqueeze(1).broadcast_to([128, 4, 128]),
                           ALU.mult, (("tmp", 0), "mlg"), ("W",))
                    hb3 = hbuf[:, 0:NT * 128].rearrange("p (j c) -> p j c", c=128)
                    if h + 1 < 4:
                        nxt_hw = load_head(h + 1)

                    def c_A(j):
                        tcols = slice(j * 128, (j + 1) * 128)
                        sl = j % 2
                        pss = ps_s2[sl][:, 0:128]
                        ptt = ps_t2[sl][:, 0, :]
                        mm(pss, kT[:, tcols], qT[:, tcols], True, True,
                           (("KT", j // 4), ("QT", j // 4)), (("pss", sl),))
                        tr(ptt, kT[:, tcols], ident_b, (("KT", j // 4), "cb16"), (("pstl", sl),))
                        cp("act", Ktok[sl][:], ptt, (("pstl", sl),), (("Ktok", sl),))
                        cp("act", S0m[sl][:], pss, (("pss", sl),), (("S0m", sl),))
                        tt("pool", S0m[sl][:], S0m[sl][:], mmk_b, ALU.mult, (("S0m", sl), "cb16"), (("S0m", sl),))

                    def c_A2(j):
                        sl = j % 2
                        pn = ps_n2[sl]
                        mm(ps_c2[sl][:, 0:129], Ktok[sl][:], Vp[:, j, :], True, True,
                           (("Ktok", sl), "Vp"), (("psc", sl),))
                        mm(pn[:, 0:129], S0m[sl][:], Vp[:, j, :], True, j == 0,
                           (("S0m", sl), "Vp"), (("psn", sl),))

                    def c_B1(j):
                        if j > 0:
                            tcols = slice(j * 128, (j + 1) * 128)
                            mm(ps_n2[j % 2][:, 0:129], qT[:, tcols], Cbf[j % 2][:], False, True,
                               (("QT", j // 4), ("Cbf", j % 2)), (("psn", j % 2),))

                    def c_B2(j):
                        gcol = h * NT + j
                        sl = j % 2
                        pn = ps_n2[sl]
                        pc = ps_c2[sl][:, 0:129]
                        if j == 0:
                            cp("dve", U[:], pc, (("psc", sl),), ("U",))
                        else:
                            stt(U[:], U[:], gcb[:, gcol - 1:gcol], pc, ALU.mult, ALU.add,
                                ("U", ("psc", sl), "gcb"), ("U",))
                        if j < NT - 1:
                            ts("dve", Cbf[(j + 1) % 2][:], U[:], gcb[:, gcol:gcol + 1], None, ALU.mult, None,
                               ("U", "gcb"), (("Cbf", (j + 1) % 2),))
                        cp("act", hb3[:, j, :], pn[:, 0:128], (("psn", sl),), ("hbuf",))
                        cp("act", den[:, j:j + 1], pn[:, 128:129], (("psn", sl),), ("den",))

                    for j in range(-1, NT):
                        if j + 1 < NT:
                            c_A(j + 1)
                        if j >= 0:
                            c_B1(j)
                        if j + 1 < NT:
                            c_A2(j + 1)
                        if j >= 0:
                            c_B2(j)
                    for t0 in range(0, NT, 4):
                        hv = hb3[:, t0:t0 + 4, :]
                        tt("dve", tmpa[:].rearrange("p (a c) -> p a c", a=4), hv, hv, ALU.mult,
                           ("hbuf",), (("tmp", 0),))
                        P.op("dve", lambda e, t0=t0: e.reduce_sum(
                            ssq[:, t0:t0 + 4], tmpa[:].rearrange("p (a c) -> p a c", a=4), AX.X),
                            (("tmp", 0),), ("ssq",))
                    Dv = den[:, 0:NT]
                    dt_ = den[:, NT:2 * NT]
                    ts("dve", dt_, Dv, -1.0, None, ALU.mult, None, ("den",), ("den2",))
                    tt("dve", dt_, dt_, Dv, ALU.max, ("den", "den2"), ("den2",))
                    tt("dve", dt_, dt_, thr[:, h * NT:(h + 1) * NT], ALU.max, ("den2", "thr"), ("den2",))
                    P.op("dve", lambda e: e.reciprocal(dt_, dt_), ("den2",), ("den2",))
                    tt("dve", ssq[:, NT:2 * NT], dt_, dt_, ALU.mult, ("den2",), ("ssq1",))
                    tt("dve", ssq[:, NT:2 * NT], ssq[:, NT:2 * NT], ssq[:, 0:NT], ALU.mult, ("ssq1", "ssq"), ("ssq1",))
                    act(ssq[:, NT:2 * NT], ssq[:, NT:2 * NT], AF.Ln, ("ssq1",), ("ssq1",), bias=EPS, scale=1.0 / 128)
                    act(ssq[:, 2 * NT:3 * NT], ssq[:, NT:2 * NT], AF.Exp, ("ssq1",), ("ssq2",), scale=-0.5)
                    tt("dve", ssq[:, 2 * NT:3 * NT], ssq[:, 2 * NT:3 * NT], dt_, ALU.mult, ("ssq2", "den2"), ("ssq2",))
                    for t0 in range(0, NT, 4):
                        hv = hb3[:, t0:t0 + 4, :]
                        tt("dve", tmpa[:].rearrange("p (a c) -> p a c", a=4), hv,
                           ssq[:, 2 * NT + t0:2 * NT + t0 + 4].unsqueeze(2).broadcast_to([128, 4, 128]),
                           ALU.mult, ("hbuf", "ssq2"), (("tmp", 0),))
                        tt("dve", tmpa[:].rearrange("p (a c) -> p a c", a=4),
                           tmpa[:].rearrange("p (a c) -> p a c", a=4), W[:, t0:t0 + 4, :], ALU.mult,
                           (("tmp", 0), "W"), (("tmp", 0),))
                        for ti in range(4):
                            tr(ps_t[:, ti, :], tmpa[:, ti * 128:(ti + 1) * 128], ident_b,
                               (("tmp", 0), "cb16"), (("pst", ti),))
                        cp("act", yT[:, 4 + h, t0 * 128:(t0 + 4) * 128],
                           ps_t[:, 0:4, :].rearrange("p a c -> p (a c)"),
                           tuple(("pst", ti) for ti in range(4)), (("yT", 4 + h, t0 // 4),))
            P.barrier()
        else:
            for p in range(4, 8):
                memset("pool", yT[:, p, :], 0.0, (("yT", p, 0),))

        arena_reset()
        d_it, d_flush = phase_D(b) if b + 1 < NSEQ else phase_D(b, nd=5)
        if b + 1 < NSEQ:
            a_it, a_setup = phase_A(b + 1)
            a_setup()
            for i in range(NT + 3):
                a_it(i - 3)
                if i < NT:
                    d_it(i)
        else:
            for i in range(NT):
                d_it(i)
        d_flush()
        P.barrier()

    P.emit()
    es.close()
    return nc


def make_consts(NT):
    p = np.arange(128)[:, None]
    f = np.arange(128)[None, :]
    c = np.zeros((128, 8, 128), np.float32)
    c[:, 0] = (p == f)
    c[:, 1] = (p >= f)
    c[:, 2] = (p < f)
    c[:, 3] = (p <= f)
    c[:, 4] = (p // 64 == f // 64)
    c[:, 5] = 1.0
    c[:, 6] = ((p // NT) == (f // NT)) & ((p % NT) < (f % NT))
    return c


def make_in_maps(inputs, S, NSEQ, n_cores):
    f32 = np.float32
    x = np.asarray(inputs["x"], f32)
    c = np.asarray(inputs["c"], f32)
    w_ada = np.ascontiguousarray(np.asarray(inputs["w_ada"], f32)[0])
    b_ada = np.asarray(inputs["b_ada"], f32)[0]
    norm_gain = np.asarray(inputs["norm_gain"], f32)[0]
    w_in = np.ascontiguousarray(np.asarray(inputs["w_in"], f32)[0])
    b_gates = np.asarray(inputs["b_gates"], f32)[0]
    qg = np.asarray(inputs["q_norm_gain"], f32)[0]
    kg = np.asarray(inputs["k_norm_gain"], f32)[0]
    conv_w = np.asarray(inputs["conv_w"], f32)[0]
    conv_b = np.asarray(inputs["conv_b"], f32)[0]
    mlg = np.asarray(inputs["ml_norm_gain"], f32)[0]
    w_out = np.ascontiguousarray(np.asarray(inputs["w_out"], f32)[0])
    NT = S // 128
    shared = {
        "w_ada": w_ada,
        "b_ada_fm": np.ascontiguousarray(b_ada.reshape(24, 128).T),
        "b_ada_gate": np.ascontiguousarray(b_ada[2048:3072].reshape(1, D)),
        "norm_gain_fm": np.ascontiguousarray(norm_gain.reshape(8, 128).T),
        "w_in": w_in,
        "b_gates_bc": np.ascontiguousarray(np.broadcast_to(b_gates[None, :], (128, 8))),
        "q_gain_fm": np.ascontiguousarray(np.tile(qg, 2).reshape(128, 1)),
        "k_gain_fm": np.ascontiguousarray(np.tile(kg, 2).reshape(128, 1)),
        "conv_w_fm": np.ascontiguousarray(conv_w.reshape(4, 8, 128).transpose(2, 1, 0)),
        "conv_b_fm": np.ascontiguousarray(conv_b.reshape(8, 128).T),
        "ml_gain_bc": np.ascontiguousarray(np.broadcast_to(mlg[None, :], (128, 512))),
        "w_out": w_out,
        "consts": make_consts(NT),
    }
    maps = []
    for i in range(n_cores):
        m = dict(shared)
        m["x"] = np.ascontiguousarray(x[i * NSEQ:(i + 1) * NSEQ])
        cc = c[i * NSEQ:(i + 1) * NSEQ]
        m["cT"] = np.ascontiguousarray(cc.reshape(NSEQ, 8, 128).transpose(2, 1, 0))
        maps.append(m)
    return maps


_NC_CACHE = {}


def kernel(**inputs):
    x = np.asarray(inputs["x"])
    B, S, _ = x.shape
    NSEQ = B // N_CORES
    key = (S, NSEQ)
    if key not in _NC_CACHE:
        _NC_CACHE[key] = build(S, NSEQ)
    nc = _NC_CACHE[key]
    in_maps = make_in_maps(inputs, S, NSEQ, N_CORES)
    res = run_bass_kernel_spmd(nc, in_maps, core_ids=list(range(N_CORES)))
    out = np.concatenate([np.asarray(r["out"], np.float32) for r in res.results], axis=0)
    return out.reshape(B, S, D).astype(np.float32)
```

```python
import math
from contextlib import ExitStack

import numpy as np
import concourse.bass as bass
import concourse.mybir as mybir
from concourse.bass_utils import run_bass_kernel_spmd

F32 = mybir.dt.float32
BF16 = mybir.dt.bfloat16
AF = mybir.ActivationFunctionType
ALU = mybir.AluOpType
AX = mybir.AxisListType

D = 1024
IN_W = 4616
EPS = 1e-6
N_CORES = 8


class Prog:
    ENG = ("pe", "act", "dve", "pool", "sp")

    def __init__(self, nc):
        self.nc = nc
        self.streams = {e: [] for e in self.ENG}
        self.cnt = {e: 0 for e in self.ENG}
        self.waited = {e: {} for e in self.ENG}
        self.writers = {}
        self.readers = {}
        self.dma_cnt = {}

    def op(self, eng, fn, reads=(), writes=(), dma=None):
        if any(isinstance(k, tuple) and len(k) == 2 and k[0] == "hnT" for k in reads):
            r2 = []
            for k in reads:
                if isinstance(k, tuple) and len(k) == 2 and k[0] == "hnT":
                    r2.extend(("hnT", k[1], kc) for kc in range(8))
                else:
                    r2.append(k)
            reads = tuple(r2)
        deps = {}

        def add(tok, same_ok):
            name, val, peng, isd = tok
            if (not isd) and peng == eng and not same_ok:
                return
            if deps.get(name, 0) < val:
                deps[name] = val

        for k in reads:
            for t in self.writers.get(k, {}).values():
                add(t, True)
        same_w = eng != "pe"
        for k in writes:
            sw = same_w and not (isinstance(k, tuple) and k[0] == "BANK")
            for t in self.writers.get(k, {}).values():
                add(t, sw)
            for t in self.readers.get(k, {}).values():
                add(t, sw)
        w = self.waited[eng]
        waits = []
        for name, val in deps.items():
            if w.get(name, 0) < val:
                waits.append((name, val))
                w[name] = val
        if dma is not None:
            self.dma_cnt[dma] = self.dma_cnt.get(dma, 0) + 16
            tok = (dma, self.dma_cnt[dma], eng, True)
        else:
            self.cnt[eng] += 1
            tok = ("E_" + eng, self.cnt[eng], eng, False)
        for k in writes:
            self.writers.setdefault(k, {})[tok[0]] = tok
            self.readers[k] = {}
        for k in reads:
            if k in writes:
                continue
            self.readers.setdefault(k, {})[tok[0]] = tok
        self.streams[eng].append((waits, fn, tok))

    def barrier(self):
        for e in self.ENG:
            waits = []
            w = self.waited[e]
            for f in self.ENG:
                if f != e and self.cnt[f] > w.get("E_" + f, 0):
                    waits.append(("E_" + f, self.cnt[f]))
                    w["E_" + f] = self.cnt[f]
            for name, val in self.dma_cnt.items():
                if val > w.get(name, 0):
                    waits.append((name, val))
                    w[name] = val
            if waits:
                self.streams[e].append((waits, None, None))

    def emit(self):
        nc = self.nc
        self.barrier()
        names = ["E_" + e for e in self.ENG] + sorted(self.dma_cnt.keys())
        with ExitStack() as es:
            sems = {n: es.enter_context(nc.semaphore(n)) for n in names}
            block = es.enter_context(nc.Block())

            def run(engname):
                def f(e):
                    for waits, fn, tok in self.streams[engname]:
                        for (n, v) in waits:
                            e.wait_ge(sems[n], v)
                        if fn is None:
                            continue
                        ins = fn(e)
                        ins.then_inc(sems[tok[0]], 16 if tok[3] else 1)
                return f

            block.tensor(run("pe"))
            block.scalar(run("act"))
            block.vector(run("dve"))
            block.gpsimd(run("pool"))
            block.sync(run("sp"))


def build(S=4096, NSEQ=2, do_attn=True, do_mlstm=True):
    nc = bass.Bass("TRN2", target_bir_lowering=False)
    P = Prog(nc)
    NT = S // 128
    NBK = S // 512
    HN = 4 * NT
    assert HN <= 128

    def dram(name, shape, kind="ExternalInput"):
        return nc.dram_tensor(name, list(shape), F32, kind=kind).ap()

    x_d = dram("x", [NSEQ, S, D])
    out_d = dram("out", [NSEQ, S, D], kind="ExternalOutput")
    cT_d = dram("cT", [128, 8, NSEQ])
    wada_d = dram("w_ada", [D, 3 * D])
    bada_fm_d = dram("b_ada_fm", [128, 24])
    bada_gate_d = dram("b_ada_gate", [1, D])
    ngain_d = dram("norm_gain_fm", [128, 8])
    win_d = dram("w_in", [D, IN_W])
    bg_d = dram("b_gates_bc", [128, 8])
    qg_d = dram("q_gain_fm", [128, 1])
    kg_d = dram("k_gain_fm", [128, 1])
    cw_d = dram("conv_w_fm", [128, 8, 4])
    cb_d = dram("conv_b_fm", [128, 8])
    mlg_d = dram("ml_gain_bc", [128, 512])
    wout_d = dram("w_out", [D, D])
    const_d = dram("consts", [128, 8, 128])

    def bk(*aps):
        keys = []
        for ap in aps:
            t = getattr(ap, "tensor", None)
            if t is None or getattr(t, "name", None) != "psb_all":
                continue
            pstride = t.shape[1]
            bank_elems = pstride // 8
            off = ap.offset % pstride
            span = 0
            for step, cnt in list(ap.ap)[1:]:
                span += (cnt - 1) * abs(step)
            for bnk in range(off // bank_elems, (off + span) // bank_elems + 1):
                k = ("BANK", bnk)
                if k not in keys:
                    keys.append(k)
        return tuple(keys)

    def mm(out, lhsT, rhs, start, stop, reads, writes, skip=False):
        P.op("pe", lambda e: e.matmul(out, lhsT, rhs, start=start, stop=stop,
                                      skip_group_check=skip), reads, tuple(writes) + bk(out))

    def tr(out, in_, ident, reads, writes):
        P.op("pe", lambda e: e.transpose(out, in_, ident), reads, tuple(writes) + bk(out))

    def act(out, in_, func, reads, writes, bias=None, scale=None):
        kw = {}
        if bias is not None:
            kw["bias"] = bias
        if scale is not None:
            kw["scale"] = scale
        P.op("act", lambda e: e.activation(out, in_, func, **kw), reads, tuple(writes) + bk(out, in_))

    def tt(eng, out, in0, in1, op, reads, writes):
        P.op(eng, lambda e: e.tensor_tensor(out, in0, in1, op), reads, tuple(writes) + bk(out, in0, in1))

    def ts(eng, out, in0, s1, s2, op0, op1, reads, writes):
        w2 = tuple(writes) + bk(out, in0, s1, s2)
        if s2 is None:
            P.op(eng, lambda e: e.tensor_scalar(out, in0, s1, None, op0), reads, w2)
        else:
            P.op(eng, lambda e: e.tensor_scalar(out, in0, s1, s2, op0, op1), reads, w2)

    def stt(out, in0, scalar, in1, op0, op1, reads, writes):
        P.op("dve", lambda e: e.scalar_tensor_tensor(out, in0, scalar, in1, op0, op1),
             reads, tuple(writes) + bk(out, in0, scalar, in1))

    def cp(eng, out, in_, reads, writes):
        w2 = tuple(writes) + bk(out, in_)
        if eng == "act":
            P.op("act", lambda e: e.copy(out, in_), reads, w2)
        else:
            P.op(eng, lambda e: e.tensor_copy(out, in_), reads, w2)

    def dveop(fn, aps, reads, writes):
        P.op("dve", fn, reads, tuple(writes) + bk(*aps))

    def memset(eng, ap, val, writes):
        P.op(eng, lambda e: e.memset(ap, val), (), writes)

    def dma(eng, out, in_, reads, writes, sem):
        P.op(eng, lambda e: e.dma_start(out=out, in_=in_), reads, writes, dma=sem)

    es = ExitStack()

    def sb(name, shape, dt=F32):
        return es.enter_context(nc.sbuf_tensor(name, list(shape), dt))

    hnT = sb("hnT", [128, 8, S], BF16)
    yT = sb("yT", [128, 8, S], BF16)
    NSTG = 3
    stg = [sb(f"stg{i}", [128, 1024], F32) for i in range(NSTG)]
    NWBF = 5
    wbf = [sb(f"wbf{i}", [128, 8, 128], BF16) for i in range(NWBF)]
    bufA = sb("bufA", [128, S], BF16)
    bufB = sb("bufB", [128, S], BF16)
    bufC = sb("bufC", [128, NT, 128], BF16)
    cf = sb("cf", [128, 4, 128], F32)
    cb16 = sb("cb16", [128, 8, 128], BF16)
    smalls = sb("smalls", [128, 64], F32)
    cw_sb = sb("cw_sb", [128, 8, 4], F32)
    mlg_sb = sb("mlg_sb", [128, 128], F32)

    C_ID, C_TM, C_MA, C_MM, C_BO, C_ON, C_PM = 0, 1, 2, 3, 4, 5, 6
    c_sc = slice(0, 8)
    c_mult = slice(8, 16)
    c_shift = slice(16, 24)
    c_bada = slice(24, 48)
    c_ng = slice(48, 56)
    c_gq = slice(56, 57)
    c_gk = slice(57, 58)
    c_bg = slice(58, 66 - 2)
    bg_sb = sb("bg_sb", [128, 8], F32)
    cbv_sb = sb("cbv_sb", [128, 8], F32)
    cT_sb = sb("cT_sb", [128, 8, NSEQ], F32)

    ARENA_F32 = 7280 if S >= 4096 else 14000
    arena = sb("arena", [128, ARENA_F32], F32)
    arena_off = [0]
    psb_all = es.enter_context(nc.psum_tensor("psb_all", [128, 4096], F32))
    psb = [psb_all[:, i * 512:(i + 1) * 512] for i in range(8)]
    psb_i = [0]

    def _shape_view(v, shape):
        if len(shape) == 2:
            return v
        if len(shape) == 3:
            return v.rearrange("p (a c) -> p a c", a=shape[1])
        raise ValueError(shape)

    class _View:
        def __init__(self, ap):
            self.ap = ap

        def __getitem__(self, k):
            return self.ap[k]

    def arena_reset():
        arena_off[0] = 0
        psb_i[0] = 0

    def lsb_arena(name, shape, dt=F32):
        esz = 4 if dt == F32 else 2
        n = 1
        for d in shape[1:]:
            n *= d
        nbytes = (n * esz + 31) // 32 * 32
        off = arena_off[0]
        assert off + nbytes <= ARENA_F32 * 4, (name, off, nbytes)
        arena_off[0] = off + nbytes
        v = arena[0:shape[0], off // 4:(off + nbytes) // 4]
        if dt != F32:
            v = v.bitcast(dt)
        v = v[:, 0:n]
        return _View(_shape_view(v, shape))

    def psum_arena(name, shape, dt=F32, bank=None):
        if bank is None:
            i = psb_i[0]
            psb_i[0] += 1
        else:
            i = bank
        assert i < 8
        v = psb[i]
        if dt != F32:
            v = v.bitcast(dt)
        n = 1
        for d in shape[1:]:
            n *= d
        v = v[0:shape[0], 0:n]
        r = _View(_shape_view(v, shape))
        r.bank = i
        return r

    stg_i = [0]

    def stg_next():
        i = stg_i[0] % NSTG
        stg_i[0] += 1
        return stg[i], ("stg", i), f"D_stg{i}"

    wbf_i = [0]

    def wbf_next():
        i = wbf_i[0] % NWBF
        wbf_i[0] += 1
        return wbf[i], ("wbf", i)

    cast_i = [0]

    def load_w_slice(col0, ncols=128):
        s, sk, ssem = stg_next()
        w, wk = wbf_next()
        sv = s[:, 0:8 * ncols].rearrange("p (k c) -> p k c", k=8)
        dma("sp", sv, win_d[:, col0:col0 + ncols].rearrange("(k p) c -> p k c", p=128),
            (), (sk,), ssem)
        eng = "pool" if cast_i[0] % 2 == 0 else "dve"
        cast_i[0] += 1
        cp(eng, w[:, :, 0:ncols], sv, (sk,), (wk,))
        return w, wk

    for ci, src in enumerate((0, 5, 3, 6)):
        dma("sp", cf[:, ci, :], const_d[:, src, :], (), (("cfp", ci),), f"D_cf{ci}")
    s0, s0k, s0sem = stg_next()
    dma("sp", s0[:], const_d.rearrange("p a c -> p (a c)"), (), (s0k,), s0sem)
    cp("dve", cb16[:].rearrange("p a c -> p (a c)"), s0[:], (s0k,), ("cb16",))
    cp("dve", smalls[:, 62:63], cf[:, 0, 0:1], tuple(("cfp", ci) for ci in range(4)), ("cf",))
    ts("dve", cb16[:, 5, :], cb16[:, C_TM, :], -1.0, None, ALU.mult, None, ("cb16",), ("cb16",))
    ts("dve", cb16[:, 7, :], cb16[:, C_MA, :], -1.0, None, ALU.mult, None, ("cb16",), ("cb16",))
    dma("sp", smalls[:, c_bada], bada_fm_d, (), ("sm_bada",), "D_sm1")
    dma("sp", smalls[:, c_ng], ngain_d, (), ("sm_ng",), "D_sm2")
    dma("sp", smalls[:, 60:61], qg_d, (), ("sm_gq0",), "D_sm3")
    dma("sp", smalls[:, 61:62], kg_d, (), ("sm_gk",), "D_sm4")
    dma("sp", bg_sb[:], bg_d, (), ("bg",), "D_bg")
    dma("sp", cbv_sb[:], cb_d, (), ("cbv",), "D_cbv")
    dma("sp", cw_sb[:], cw_d, (), ("cw",), "D_cw")
    dma("sp", cT_sb[:], cT_d, (), ("cT",), "D_cT")
    ts("dve", smalls[:, c_gq], smalls[:, 60:61], 0.125, None, ALU.mult, None, ("sm_gq0",), ("sm_gq",))
    gq_ap = smalls[:, c_gq]
    gk_ap = smalls[:, 61:62]

    ident_f = cf[:, 0, :]
    ones_f = cf[:, 1, :]
    tinc_f = cf[:, 2, :]
    pm_f = cf[:, 3, :]
    ident_b = cb16[:, C_ID, :]
    tm_b = cb16[:, C_TM, :]
    ma_b = cb16[:, C_MA, :]
    mmk_b = cb16[:, C_MM, :]
    bo_b = cb16[:, C_BO, :]
    ntm_b = cb16[:, 5, :]
    nma_b = cb16[:, 7, :]

    P.barrier()

    def phase_A(b, n_xslots=0):
        psum, lsb = psum_arena, lsb_arena

        ps_mod = psum(f"psmod{b}", [128, 512])
        ps_tl = [psum(f"pstl{b}_{i}", [128, 8, 128], BF16) for i in range(2)]
        ps_th = [psum(f"psth{b}_{i}", [128, 8, 128], BF16) for i in range(2)]
        junk2 = [lsb(f"junk{b}_{i}", [128, 1024], BF16) for i in range(1)] * 2
        xn = [lsb(f"xn{b}_{i}", [128, 1024], BF16) for i in range(2)]
        st = lsb(f"st{b}", [128, 4 * NT], F32)
        etmp = lsb(f"etmp{b}", [128, 8], F32)

        def a_setup():
            act(etmp[:], cT_sb[:, :, b], AF.Exp, ("cT",), ("etmp",), scale=-1.0)
            ts("dve", etmp[:], etmp[:], 1.0, None, ALU.add, None, ("etmp",), ("etmp",))
            P.op("dve", lambda e: e.reciprocal(etmp[:], etmp[:]), ("etmp",), ("etmp",))
            tt("dve", smalls[:, c_sc], etmp[:], cT_sb[:, :, b], ALU.mult, ("etmp", "cT"), ("sm_sc",))

            first = True
            for kc in range(8):
                for piece in range(2):
                    s, sk, ssem = stg_next()
                    dma("sp", s[:], wada_d[kc * 128:(kc + 1) * 128, piece * 1024:(piece + 1) * 1024],
                        (), (sk,), ssem)
                    for f8 in range(8):
                        ft = piece * 8 + f8
                        mm(ps_mod[:, ft:ft + 1], s[:, f8 * 128:(f8 + 1) * 128], smalls[:, kc:kc + 1],
                           first, (kc == 7 and ft == 15), (sk, "sm_sc"), ("psmod",), skip=True)
                        first = False
            tt("dve", smalls[:, c_shift], ps_mod[:, 0:8], smalls[:, 24:32], ALU.add,
               ("psmod", "sm_bada"), ("sm_shift",))
            tt("dve", smalls[:, c_mult], ps_mod[:, 8:16], smalls[:, 32:40], ALU.add,
               ("psmod", "sm_bada"), ("sm_mult",))
            stt(smalls[:, c_mult], smalls[:, c_mult], 1.0, smalls[:, c_ng], ALU.add, ALU.mult,
                ("sm_mult", "sm_ng"), ("sm_mult",))


        xs = {}
        xsl = [lsb(f"xsl{b}_{i}", [128, 1024], F32) for i in range(n_xslots)]
        LA = (n_xslots - 1) if n_xslots else 3

        def a_s1a(t):
            if n_xslots:
                i = t % n_xslots
                s, sk, ssem = xsl[i], ("xsl", i), f"D_xsl{i}"
            else:
                s, sk, ssem = stg_next()
            xs[t] = (s, sk)
            dma("sp", s[:], x_d[b, t * 128:(t + 1) * 128, :], (), (sk,), ssem)
            junk = junk2[t % 2]
            act(junk[:], s[:], AF.Square, (sk,), ("junk",))
            P.op("dve", lambda e, o=st[:, 4 * t:4 * t + 1], junk=junk: e.reduce_sum(o, junk[:], AX.X),
                 ("junk",), (("st", t),))

        def a_s1b(t):
            s, sk = xs[t]
            act(st[:, 4 * t + 1:4 * t + 2], st[:, 4 * t:4 * t + 1], AF.Ln, (("st", t),), (("st1", t),),
                bias=EPS, scale=1.0 / D)
            act(st[:, 4 * t + 2:4 * t + 3], st[:, 4 * t + 1:4 * t + 2], AF.Exp, (("st1", t),),
                (("st2", t),), scale=-0.5)
            ts("dve", xn[t % 2][:], s[:], st[:, 4 * t + 2:4 * t + 3], None, ALU.mult, None,
               (sk, ("st2", t)), (("xn", t % 2),))

        def a_s2(t):
            xb = xn[t % 2]
            for kc in range(8):
                pt = (ps_tl if kc < 4 else ps_th)[t % 2]
                tr(pt[:, kc % 4, :], xb[:, kc * 128:(kc + 1) * 128], ident_b,
                   (("xn", t % 2), "cb16"), (("pst", t % 2, kc // 4),))
            for kc in range(8):
                o = hnT[:, kc, t * 128:(t + 1) * 128]
                if kc < 4:
                    ts("dve", o, ps_tl[t % 2][:, kc, :], smalls[:, 8 + kc:9 + kc], smalls[:, 16 + kc:17 + kc],
                       ALU.mult, ALU.add, (("pst", t % 2, 0), "sm_mult", "sm_shift"), (("hnT", t // 4, kc),))
                else:
                    act(o, ps_th[t % 2][:, kc - 4, :], AF.Identity, (("pst", t % 2, 1), "sm_mult", "sm_shift"),
                        (("hnT", t // 4, kc),), bias=smalls[:, 16 + kc:17 + kc],
                        scale=smalls[:, 8 + kc:9 + kc])

        def a_iter(t):
            if 0 <= t + LA < NT:
                a_s1a(t + LA)
            if 0 <= t + 1 < NT:
                a_s1b(t + 1)
            if 0 <= t < NT:
                a_s2(t)
        return a_iter, a_setup

    def phase_D(b, nd=3, store_eng="act"):
        psum, lsb = psum_arena, lsb_arena

        ps_x = [psum(f"dpsx{b}_{i}", [128, 512]) for i in range(2)]
        ps_gt = ps_x
        tmpD = lsb(f"tmpD{b}", [128, 1024], F32)
        scbc = tmpD[:].rearrange("p (k c) -> p k c", k=8)
        gate = lsb(f"gate{b}", [128, D], F32)
        if S >= 4096:
            def wo(kc):
                return (bufA if kc < 4 else bufB)[:, (kc % 4) * 1024:(kc % 4 + 1) * 1024]
        else:
            wo_ar = lsb(f"wobf{b}", [128, 8, D], BF16)

            def wo(kc):
                return wo_ar[:, kc, :]
        dsl = [lsb(f"dsl{b}_{i}", [128, 1024], F32) for i in range(nd)]
        otmp = [tmpD[:, 0:512], tmpD[:, 512:1024]]

        cp("dve", scbc, smalls[:, c_sc].unsqueeze(2).broadcast_to([128, 8, 128]), ("sm_sc",), ("scbc",))
        sbr, sbrk, sbrsem = stg_next()
        dma("sp", sbr[0:1, :], bada_gate_d, (), (sbrk,), sbrsem)
        for hf in range(2):
            mm(ps_gt[hf][:], ones_f[0:1, :], sbr[0:1, hf * 512:(hf + 1) * 512], True, False,
               (sbrk, "cf"), (("psgt", hf),))
        for kc in range(8):
            s, sk, ssem = stg_next()
            dma("sp", s[:], wada_d[kc * 128:(kc + 1) * 128, 2048:3072], (), (sk,), ssem)
            for hf in range(2):
                mm(ps_gt[hf][:], scbc[:, kc, :], s[:, hf * 512:(hf + 1) * 512], False, kc == 7,
                   ("scbc", sk), (("psgt", hf),))
        for hf in range(2):
            cp("dve", gate[:, hf * 512:(hf + 1) * 512], ps_gt[hf][:], (("psgt", hf),), ("gate",))
        for kc in range(8):
            s, sk, ssem = stg_next()
            dma("sp", s[:], wout_d[kc * 128:(kc + 1) * 128, :], (), (sk,), ssem)
            cp("pool" if kc % 2 == 0 else "dve", wo(kc), s[:], (sk,), ("wobf",))
        ykeys = lambda t: tuple(("yT", p, t // 4) for p in range(8)) + tuple(("yT", p, 0) for p in range(8))
        pend = []

        def d_iter(t):
            i = t % nd
            s, sk, ssem = dsl[i], ("dsl", i), f"D_dsl{i}"
            dma("sp", s[:], x_d[b, t * 128:(t + 1) * 128, :], (), (sk,), ssem)
            for hf in range(2):
                px = ps_x[hf]
                for kc in range(8):
                    mm(px[:], yT[:, kc, t * 128:(t + 1) * 128], wo(kc)[:, hf * 512:(hf + 1) * 512],
                       kc == 0, kc == 7, ykeys(t) + ("wobf",), (("psx", hf),))
                tt("dve", otmp[hf], px[:], gate[:, hf * 512:(hf + 1) * 512], ALU.mult,
                   (("psx", hf), "gate"), (("otmp", hf), "scbc"))
                tt("pool", s[:, hf * 512:(hf + 1) * 512], s[:, hf * 512:(hf + 1) * 512], otmp[hf], ALU.add,
                   (sk, ("otmp", hf)), (sk,))
            if pend:
                pend.pop(0)()
            pend.append(lambda s=s, sk=sk, i=i, t=t:
                        dma(store_eng, out_d[b, t * 128:(t + 1) * 128, :], s[:], (sk,), (), f"D_dst{i}"))

        def d_flush():
            while pend:
                pend.pop(0)()
        return d_iter, d_flush

    arena_reset()
    a_it, a_setup = phase_A(0, n_xslots=5)
    for t in range(-4, 0):
        a_it(t)
    a_setup()
    for t in range(0, NT):
        a_it(t)
    P.barrier()

    for b in range(NSEQ):

        if do_attn:
            with ExitStack() as ph:
                arena_reset()
                lsb = lsb_arena
                zz = psb_all[:, 0:1024].rearrange("p (h c) -> p h c", h=2)
                gg = psb_all[:, 2048:3072].rearrange("p (h c) -> p h c", h=2)
                ps_x = [psb[2], psb[3], psb[0]]
                ps_gp = [psb[4], psb[5]]
                ps_o = [psb[6], psb[7]]
                Pb = [lsb(f"Pb{b}_{i}", [128, 2, 512], F32) for i in range(2)]
                Lb = [lsb(f"Lb{b}_{i}", [128, 2, 512], BF16) for i in range(3)]
                KTn = [lsb(f"KTn{b}_{i}", [128, 128], BF16) for i in range(3)]
                Ab = [lsb(f"Ab{b}_{i}", [128, 2, 512], BF16) for i in range(2)]
                SZ = [lsb(f"SZ{b}_{i}", [128, 512], BF16) for i in range(2)]
                ztmp = lsb(f"ztmp{b}", [128, 512], F32)
                QTz = [lsb(f"QTz{b}_{i}", [128, 2, 512], BF16) for i in range(2)]
                QT, KT, V = bufA, bufB, bufC
                for sl in range(2):
                    memset("pool", QTz[sl][:], 0.0, (("QTz", sl),))

                def load_pair(p):
                    return (load_w_slice(0 + 128 * p), load_w_slice(512 + 128 * p),
                            load_w_slice(1024 + 128 * p), load_w_slice(1536 + 128 * p))

                nxt_w = load_pair(0)
                for p in range(4):
                    (wq, wqk), (wk_, wkk), (wv, wvk), (wz, wzk) = nxt_w

                    jobs = []
                    for (wsl, wkey, dst, dkey, gain, gkey) in ((wq, wqk, QT, "QT", gq_ap, "sm_gq"),
                                                              (wk_, wkk, KT, "KT", gk_ap, "sm_gk")):
                        for blk in range(NBK):
                            jobs.append((wsl, wkey, dst, dkey, gain, gkey, blk))
                    NJ = len(jobs)

                    def pj_mm(n):
                        wsl, wkey, dst, dkey, gain, gkey, blk = jobs[n]
                        px = ps_x[n % 3]
                        cols = slice(blk * 512, (blk + 1) * 512)
                        for kc in range(8):
                            mm(px, wsl[:, kc, :], hnT[:, kc, cols], kc == 0, kc == 7,
                               (wkey, ("hnT", blk)), (("psx", n % 3),))

                    def pj_sq(n):
                        act(Lb[n % 2][:, 0, :], ps_x[n % 3], AF.Square, (("psx", n % 3),), (("Lb", n % 2),))
                        mm(ps_gp[n % 2], bo_b, Lb[n % 2][:, 0, :], True, True, (("Lb", n % 2), "cb16"),
                           (("gg", n % 2),))

                    def pj_fin(n):
                        wsl, wkey, dst, dkey, gain, gkey, blk = jobs[n]
                        cols = slice(blk * 512, (blk + 1) * 512)
                        r = Pb[n % 2][:, 0, :]
                        act(r, ps_gp[n % 2], AF.Ln, (("gg", n % 2),), (("Pb", n % 2),), bias=EPS, scale=1.0 / 64)
                        act(r, r, AF.Exp, (("Pb", n % 2),), (("Pb", n % 2),), scale=-0.5)
                        stt(dst[:, cols], ps_x[n % 3], gain, r, ALU.mult, ALU.mult,
                            (("psx", n % 3), ("Pb", n % 2), gkey), ((dkey, blk),))

                    for n in range(-2, NJ):
                        if 0 <= n + 2 < NJ:
                            pj_mm(n + 2)
                        if 0 <= n + 1 < NJ:
                            pj_sq(n + 1)
                        if 0 <= n:
                            pj_fin(n)

                    for t0 in range(0, NT, 4):
                        pi = (t0 // 4) % 2
                        px = ps_x[pi]
                        pxk = ("psx", pi)
                        for ti in range(4):
                            t = t0 + ti
                            for kc in range(8):
                                mm(px[:, ti * 128:(ti + 1) * 128], hnT[:, kc, t * 128:(t + 1) * 128],
                                   wv[:, kc, :], kc == 0, kc == 7, (wvk, ("hnT", t // 4)), (pxk,), skip=True)
                        cp("dve", V[:, t0:t0 + 4, :], px.rearrange("p (a c) -> p a c", a=4),
                           (pxk,), (("V", t0 // 4),))

                    if p + 1 < 4:
                        nxt_w = load_pair(p + 1)
                    steps = []
                    for qb in range(NBK):
                        nk = 4 * (qb + 1)
                        for kt in range(nk - 1, -1, -1):
                            r = kt - 4 * qb
                            c0 = 128 * r if r >= 0 else 0
                            steps.append(dict(qb=qb, kt=kt, c0=c0, diag=(r >= 0),
                                              first=(kt == nk - 1), last=(kt == 0)))
                    NS = len(steps)

                    def prep(qb):
                        cols = slice(qb * 512, (qb + 1) * 512)
                        sl = qb % 2
                        px = ps_x[sl]
                        pxk = ("psx", sl)
                        for kc in range(8):
                            mm(px, wz[:, kc, :], hnT[:, kc, cols], kc == 0, kc == 7,
                               (wzk, ("hnT", qb)), (pxk,))
                        act(ztmp[:], px, AF.Exp, (pxk,), ("ztmp",), scale=-1.0)
                        ts("dve", ztmp[:], ztmp[:], 1.0, None, ALU.add, None, ("ztmp",), ("ztmp",))
                        P.op("dve", lambda e: e.reciprocal(ztmp[:], ztmp[:]), ("ztmp",), ("ztmp",))
                        tt("dve", SZ[sl][:], px, ztmp[:], ALU.mult, (pxk, "ztmp"), (("SZ", sl),))
                        for hd in range(2):
                            pr = slice(64 * hd, 64 * hd + 64)
                            cp("pool", QTz[sl][pr, hd, :], QT[pr, cols], (("QT", qb),), (("QTz", sl),))

                    def S1(j):
                        T = steps[j]
                        if T["first"]:
                            prep(T["qb"])
                        c0, qb, kt = T["c0"], T["qb"], T["kt"]
                        for hd in range(2):
                            mm(zz[:, hd, c0:512], KT[:, kt * 128:(kt + 1) * 128], QTz[qb % 2][:, hd, c0:512],
                               True, True, (("KT", kt // 4), ("QTz", qb % 2)), ("zz", ("psx", 2)))

                    def S2p(j):
                        c0 = steps[j]["c0"]
                        act(Pb[j % 2][:, :, c0:512], zz[:, :, c0:512], AF.Exp, ("zz", ("psx", 2)), (("Pb", j % 2),))

                    def S2l(j):
                        T = steps[j]
                        c0, kt = T["c0"], T["kt"]
                        act(Lb[j % 3][:, :, c0:512], Pb[j % 2][:, :, c0:512], AF.Ln,
                            (("Pb", j % 2),), (("Lb", j % 3),), bias=1.0)
                        if T["diag"]:
                            tt("dve", Lb[j % 3][:, :, c0:c0 + 128], Lb[j % 3][:, :, c0:c0 + 128],
                               ma_b.unsqueeze(1).broadcast_to([128, 2, 128]), ALU.mult,
                               (("Lb", j % 3), "cb16"), (("Lb", j % 3),))
                        if not T["last"]:
                            ts("pool", KTn[j % 3][:], KT[:, kt * 128:(kt + 1) * 128], -1.0, None, ALU.mult, None,
                               (("KT", kt // 4),), (("KTn", j % 3),))

                    def S3(j, hd):
                        T = steps[j]
                        c0, qb, kt = T["c0"], T["qb"], T["kt"]
                        mm(gg[:, hd, c0:512], ntm_b, Lb[j % 3][:, hd, c0:512], T["first"], False,
                           (("Lb", j % 3), "cb16"), (("gg", hd),), skip=True)
                        mm(gg[:, hd, c0:512], KT[:, kt * 128:(kt + 1) * 128], QTz[qb % 2][:, hd, c0:512],
                           False, True, (("KT", kt // 4), ("QTz", qb % 2)), (("gg", hd),), skip=True)

                    def S4(j, hd):
                        T = steps[j]
                        c0 = T["c0"]
                        act(Ab[j % 2][:, hd, c0:512], gg[:, hd, c0:512], AF.Exp,
                            (("gg", hd),), (("Ab", j % 2, hd),))
                        if T["diag"]:
                            tt("dve", Ab[j % 2][:, hd, c0:c0 + 128], Ab[j % 2][:, hd, c0:c0 + 128],
                               ma_b, ALU.mult, (("Ab", j % 2, hd), "cb16"), (("Ab", j % 2, hd),))

                    def S5fix(j, hd):
                        T = steps[j]
                        c0, qb, kt = T["c0"], T["qb"], T["kt"]
                        if T["last"]:
                            return
                        mm(gg[:, hd, c0:512], KTn[j % 3][:], QTz[qb % 2][:, hd, c0:512], False, False,
                           (("KTn", j % 3), ("QTz", qb % 2)), (("gg", hd),), skip=True)
                        mm(gg[:, hd, c0:512], nma_b, Lb[j % 3][:, hd, c0:512], False, True,
                           (("Lb", j % 3), "cb16"), (("gg", hd),), skip=True)

                    def S5o(j):
                        T = steps[j]
                        c0, qb, kt = T["c0"], T["qb"], T["kt"]
                        for hd in range(2):
                            o = ps_o[hd]
                            mm(o[:, c0:512], V[:, kt, :], Ab[j % 2][:, hd, c0:512],
                               T["first"], T["last"], (("V", kt // 4), ("Ab", j % 2, hd)), (("pso", hd),), skip=True)
                            if T["last"]:
                                pb = 64 * hd
                                cols = slice(qb * 512, (qb + 1) * 512)
                                tt("dve", yT[pb:pb + 64, p, cols], o[pb:pb + 64, :], SZ[qb % 2][pb:pb + 64, :],
                                   ALU.mult, (("pso", hd), ("SZ", qb % 2)), (("yT", p, qb),))

                    for j in range(-4, NS):
                        for hd in range(2):
                            if 0 <= j:
                                S4(j, hd)
                                S5fix(j, hd)
                            if 0 <= j + 1 < NS:
                                S3(j + 1, hd)
                        if 0 <= j + 2 < NS:
                            S2l(j + 2)
                        if 0 <= j + 3 < NS:
                            S2p(j + 3)
                        if 0 <= j:
                            S5o(j)
                        if 0 <= j + 4 < NS:
                            S1(j + 4)
            P.barrier()
        else:
            for p in range(4):
                memset("pool", yT[:, p, :], 0.0, (("yT", p, 0),))

        if do_mlstm:
            with ExitStack() as ph:
                arena_reset()
                psum, lsb = psum_arena, lsb_arena

                ps_x = [psum(f"cpsx{b}_{i}", [128, 512]) for i in range(2)]
                ps_t = psum(f"cpst{b}", [128, 8, 128], BF16)
                ps_s = psum(f"cpss{b}", [128, 512])
                ps_n = psum(f"cpsn{b}", [128, 512])
                ps_n2 = [ps_n, psum(f"cpsn2{b}", [128, 512])]
                ps_c = psum(f"cpsc{b}", [128, 512])
                ps_m = psum(f"cpsm{b}", [128, 512])
                ps_s2 = [ps_s, psum(f"cpss2{b}", [128, 512], bank=ps_x[0].bank)]
                ps_c2 = [ps_c, psum(f"cpsc2{b}", [128, 512], bank=ps_x[1].bank)]
                ps_t2 = [ps_t, psum(f"cpst2{b}", [128, 8, 128], BF16, bank=ps_m.bank)]

                gi = lsb(f"gi{b}", [128, HN])
                gf = lsb(f"gf{b}", [128, HN])
                t1f = lsb(f"t1{b}", [128, 128])
                t1 = t1f[:, 0:HN]
                Bn = lsb(f"Bn{b}", [128, HN])
                lfn = gf
                Xs = t1f
                ag = gi
                cmx = lsb(f"cmx{b}", [128, 1])
                rows = lsb(f"rows{b}", [1, 3, HN])
                colfac = lsb(f"colfac{b}", [128, HN])
                thr = lsb(f"thr{b}", [128, HN])
                gcb = lsb(f"gcb{b}", [128, HN])
                Vp = lsb(f"Vp{b}", [128, NT, 129], BF16)
                hbuf = lsb(f"hbuf{b}", [128, NT * 128 + 8], BF16)
                Dg = lsb(f"Dg{b}", [128, 8, 128], BF16)
                S0m = [lsb(f"S0m{b}_{i}", [128, 128], BF16) for i in range(2)]
                Ktok = [lsb(f"Ktok{b}_{i}", [128, 128], BF16) for i in range(2)]
                U = lsb(f"U{b}", [128, 129], F32)
                Cbf = [lsb(f"Cbf{b}_{i}", [128, 129], BF16) for i in range(2)]
                den = lsb(f"den{b}", [128, 2 * NT], F32)
                ssq = lsb(f"ssq{b}", [128, 3 * NT], F32)
                tmpa = lsb(f"tmpa{b}", [128, 512], BF16)
                qT, kT, W = bufA, bufB, bufC

                s, sk, ssem = stg_next()
                wg, wgk = wbf_next()
                sv = s[:, 0:64].rearrange("p (k c) -> p k c", k=8)
                dma("sp", sv, win_d[:, 4608:4616].rearrange("(k p) c -> p k c", p=128), (), (sk,), ssem)
                cp("dve", wg[:, :, 0:8], sv, (sk,), (wgk,))
                for t in range(NT):
                    for kc in range(8):
                        mm(ps_m[:, t * 8:(t + 1) * 8], hnT[:, kc, t * 128:(t + 1) * 128], wg[:, kc, 0:8],
                           kc == 0, kc == 7, (wgk, ("hnT", t // 4)), ("psm",), skip=True)
                pm3 = ps_m[:, 0:NT * 8].rearrange("p (j c) -> p j c", c=8)
                gi3 = gi[:].rearrange("p (h j) -> p h j", h=4)
                gf3 = gf[:].rearrange("p (h j) -> p h j", h=4)
                tt("dve", gi3, pm3[:, :, 0:4].rearrange("p j h -> p h j"),
                   bg_sb[:, 0:4].unsqueeze(2).broadcast_to([128, 4, NT]), ALU.add, ("psm", "bg"), ("gi",))
                tt("dve", gf3, pm3[:, :, 4:8].rearrange("p j h -> p h j"),
                   bg_sb[:, 4:8].unsqueeze(2).broadcast_to([128, 4, NT]), ALU.add, ("psm", "bg"), ("gf",))
                act(t1[:], gf[:], AF.Exp, ("gf",), ("t1",), scale=-1.0)
                act(lfn[:], t1[:], AF.Ln, ("t1",), ("gf",), bias=1.0)
                mm(ps_s[0:HN, 0:128], lfn[:], ones_f, True, True, ("gf", "cf"), (("pss", 0), ("pss", 1),))
                cp("dve", Xs[0:HN, :], ps_s[0:HN, 0:128], (("pss", 0), ("pss", 1),), ("t1",))
                mm(ps_n[:, 0:HN], tinc_f, lfn[:], True, False, ("gf", "cf"), (("psn", 0),))
                mm(ps_n[:, 0:HN], Xs[0:HN, :], pm_f[0:HN, 0:HN], False, True, ("t1", "cf"), (("psn", 0),))
                cp("dve", Bn[:], ps_n[:, 0:HN], (("psn", 0),), ("Bn",))
                tt("dve", ag[:], gi[:], Bn[:], ALU.add, ("gi", "Bn"), ("gi",))
                tr(ps_s[0:HN, 0:128], ag[:], ident_f, ("gi", "cf"), (("pss", 0), ("pss", 1),))
                P.op("dve", lambda e: e.reduce_max(cmx[0:HN, :], ps_s[0:HN, 0:128], AX.X), (("pss", 0), ("pss", 1),),
                     ("cmx",) + bk(ps_s[0:HN, 0:128]))
                mm(ps_c[0:1, 0:HN], cmx[0:HN, :], ident_f[0:HN, 0:HN], True, True, ("cmx", "cf"), (("psc", 0), ("psc", 1),))
                cp("dve", rows[:, 0, :], ps_c[0:1, 0:HN], (("psc", 0), ("psc", 1),), ("rows0",))
                for h in range(4):
                    seg = slice(h * NT, (h + 1) * NT)
                    P.op("dve", lambda e, seg=seg: e.tensor_tensor_scan(
                        rows[:, 1, seg], rows[:, 0, seg], rows[:, 0, seg], 0.0, ALU.max, ALU.max),
                        ("rows0",), ("rows1",))
                memset("dve", rows[:, 2, :], 0.0, ("rows2",))
                if NT > 1:
                    r1 = rows[:, 1, :].rearrange("p (h j) -> p h j", h=4)
                    r2 = rows[:, 2, :].rearrange("p (h j) -> p h j", h=4)
                    cp("dve", r2[:, :, 1:NT], r1[:, :, 0:NT - 1], ("rows1", "rows2"), ("rows2",))
                mm(ps_s[:, 0:HN], ones_f[0:1, :], rows[:, 2, :], True, True, ("rows2", "cf"), (("pss", 0), ("pss", 1),))
                mm(ps_n[:, 0:HN], ones_f[0:1, :], rows[:, 1, :], True, True, ("rows1", "cf"), (("psn", 0),))
                tt("dve", t1[:], ag[:], ps_s[:, 0:HN], ALU.subtract, ("gi", ("pss", 0), ("pss", 1)), ("t1",))
                act(colfac[:], t1[:], AF.Exp, ("t1",), ("colfac",))
                tt("dve", t1[:], Bn[:], ps_s[:, 0:HN], ALU.subtract, ("Bn", ("pss", 0), ("pss", 1)), ("t1",))
                act(thr[:], t1[:], AF.Exp, ("t1",), ("thr0",))
                ts("dve", thr[:], thr[:], math.sqrt(128.0), None, ALU.mult, None, ("thr0",), ("thr",))
                cp("dve", t1[:], ps_s[:, 0:HN], (("pss", 0), ("pss", 1),), ("t1",))
                tt("dve", t1[:], t1[:], ps_n[:, 0:HN], ALU.subtract, ("t1", ("psn", 0)), ("t1",))
                act(gcb[:], t1[:], AF.Exp, ("t1",), ("gcb",))

                def load_head(h):
                    return (load_w_slice(2048 + 128 * h), load_w_slice(2560 + 128 * h),
                            load_w_slice(3072 + 128 * h), load_w_slice(3584 + 128 * h),
                            load_w_slice(4096 + 128 * h))

                nxt_hw = load_head(0)
                for h in range(4):
                    (wq, wqk), (wk_, wkk), (wv, wvk), (wo, wok), (wz, wzk) = nxt_hw
                    dma("sp", mlg_sb[:], mlg_d[:, h * 128:(h + 1) * 128], (), ("mlg",), "D_mlg")
                    for j in range(4):
                        ts("dve", Dg[:, j, :], ident_f, cw_sb[:, h, j:j + 1], None, ALU.mult, None,
                           ("cf", "cw"), ("Dg",))
                        ts("dve", Dg[:, 4 + j, :], ident_f, cw_sb[:, 4 + h, j:j + 1], None, ALU.mult, None,
                           ("cf", "cw"), ("Dg",))
                    memset("pool", hbuf[:, 0:3], 0.0, ("hbuf",))
                    for (wsl, wkey, dst, dkey, dgo, cbi) in ((wq, wqk, qT, "QT", 0, h),
                                                             (wk_, wkk, kT, "KT", 4, 4 + h)):
                        for blk in range(NBK):
                            px = ps_x[blk % 2]
                            pxk = ("psx", blk % 2)
                            cols = slice(blk * 512, (blk + 1) * 512)
                            for kc in range(8):
                                mm(px[:], wsl[:, kc, :], hnT[:, kc, cols], kc == 0, kc == 7,
                                   (wkey, ("hnT", blk)), (pxk,))
                            cp("dve", hbuf[:, 3 + blk * 512:3 + (blk + 1) * 512], px[:], (pxk,), ("hbuf",))
                        for blk in range(NBK):
                            px = ps_x[blk % 2]
                            pxk = ("psx", blk % 2)
                            cols = slice(blk * 512, (blk + 1) * 512)
                            for j in range(4):
                                mm(px[:], Dg[:, dgo + j, :], hbuf[:, blk * 512 + j:blk * 512 + j + 512],
                                   j == 0, j == 3, ("Dg", "hbuf"), (pxk,))
                            act(dst[:, cols], px[:], AF.Silu, (pxk, "cbv"), ((dkey, blk),),
                                bias=cbv_sb[:, cbi:cbi + 1])
                    for t0 in range(0, NT, 4):
                        px = ps_x[(t0 // 4) % 2]
                        pxk = ("psx", (t0 // 4) % 2)
                        for ti in range(4):
                            t = t0 + ti
                            for kc in range(8):
                                mm(px[:, ti * 128:(ti + 1) * 128], hnT[:, kc, t * 128:(t + 1) * 128],
                                   wv[:, kc, :], kc == 0, kc == 7, (wvk, ("hnT", t // 4)), (pxk,), skip=True)
                        tt("dve", Vp[:, t0:t0 + 4, 0:128], px[:].rearrange("p (a c) -> p a c", a=4),
                           colfac[:, h * NT + t0:h * NT + t0 + 4].unsqueeze(2).broadcast_to([128, 4, 128]),
                           ALU.mult, (pxk, "colfac"), ("Vp",))
                    cp("dve", Vp[:, :, 128:129], colfac[:, h * NT:(h + 1) * NT].unsqueeze(2),
                       ("colfac",), ("Vp",))
                    for t0 in range(0, NT, 4):
                        for (wsl, wkey, fn, dstt, pi) in ((wo, wok, AF.Sigmoid, tmpa, 0), (wz, wzk, AF.Silu, hbuf, 1)):
                            px = ps_x[pi]
                            pxk = ("psx", pi)
                            for ti in range(4):
                                t = t0 + ti
                                for kc in range(8):
                                    mm(px[:, ti * 128:(ti + 1) * 128], hnT[:, kc, t * 128:(t + 1) * 128],
                                       wsl[:, kc, :], kc == 0, kc == 7, (wkey, ("hnT", t // 4)), (pxk,), skip=True)
                            act(dstt[:, 0:512], px[:], fn, (pxk,), (("tmp", 0) if pi == 0 else "hbuf",))
                        tt("dve", tmpa[:], tmpa[:], hbuf[:, 0:512], ALU.mult, (("tmp", 0), "hbuf"), (("tmp", 0),))
                        tt("pool", W[:, t0:t0 + 4, :], tmpa[:].rearrange("p (a c) -> p a c", a=4),
                           mlg_sb[:].unsqueeze(1).broadcast_to([128, 4, 128]),
                           ALU.mult, (("tmp", 0), "mlg"), ("W",))
                    hb3 = hbuf[:, 0:NT * 128].rearrange("p (j c) -> p j c", c=128)
                    if h + 1 < 4:
                        nxt_hw = load_head(h + 1)

                    def c_A(j):
                        tcols = slice(j * 128, (j + 1) * 128)
                        sl = j % 2
                        pss = ps_s2[sl][:, 0:128]
                        ptt = ps_t2[sl][:, 0, :]
                        mm(pss, kT[:, tcols], qT[:, tcols], True, True,
                           (("KT", j // 4), ("QT", j // 4)), (("pss", sl),))
                        tr(ptt, kT[:, tcols], ident_b, (("KT", j // 4), "cb16"), (("pstl", sl),))
                        tt("dve", S0m[sl][:], pss, mmk_b, ALU.mult, (("pss", sl), "cb16"), (("S0m", sl),))
                        cp("act", Ktok[sl][:], ptt, (("pstl", sl),), (("Ktok", sl),))

                    def c_A2(j):
                        sl = j % 2
                        pn = ps_n2[sl]
                        mm(ps_c2[sl][:, 0:129], Ktok[sl][:], Vp[:, j, :], True, True,
                           (("Ktok", sl), "Vp"), (("psc", sl),))
                        mm(pn[:, 0:129], S0m[sl][:], Vp[:, j, :], True, j == 0,
                           (("S0m", sl), "Vp"), (("psn", sl),))

                    def c_B1(j):
                        if j > 0:
                            tcols = slice(j * 128, (j + 1) * 128)
                            mm(ps_n2[j % 2][:, 0:129], qT[:, tcols], Cbf[j % 2][:], False, True,
                               (("QT", j // 4), ("Cbf", j % 2)), (("psn", j % 2),))

                    def c_B2(j):
                        gcol = h * NT + j
                        sl = j % 2
                        pn = ps_n2[sl]
                        pc = ps_c2[sl][:, 0:129]
                        if j == 0:
                            cp("dve", U[:], pc, (("psc", sl),), ("U",))
                        else:
                            stt(U[:], U[:], gcb[:, gcol - 1:gcol], pc, ALU.mult, ALU.add,
                                ("U", ("psc", sl), "gcb"), ("U",))
                        if j < NT - 1:
                            ts("dve", Cbf[(j + 1) % 2][:], U[:], gcb[:, gcol:gcol + 1], None, ALU.mult, None,
                               ("U", "gcb"), (("Cbf", (j + 1) % 2),))
                        cp("act", hb3[:, j, :], pn[:, 0:128], (("psn", sl),), ("hbuf",))
                        cp("act", den[:, j:j + 1], pn[:, 128:129], (("psn", sl),), ("den",))

                    for j in range(-1, NT):
                        if j + 1 < NT:
                            c_A(j + 1)
                        if j >= 0:
                            c_B1(j)
                        if j + 1 < NT:
                            c_A2(j + 1)
                        if j >= 0:
                            c_B2(j)
                    for t0 in range(0, NT, 4):
                        hv = hb3[:, t0:t0 + 4, :]
                        tt("dve", tmpa[:].rearrange("p (a c) -> p a c", a=4), hv, hv, ALU.mult,
                           ("hbuf",), (("tmp", 0),))
                        P.op("dve", lambda e, t0=t0: e.reduce_sum(
                            ssq[:, t0:t0 + 4], tmpa[:].rearrange("p (a c) -> p a c", a=4), AX.X),
                            (("tmp", 0),), ("ssq",))
                    Dv = den[:, 0:NT]
                    dt_ = den[:, NT:2 * NT]
                    ts("dve", dt_, Dv, -1.0, None, ALU.mult, None, ("den",), ("den2",))
                    tt("dve", dt_, dt_, Dv, ALU.max, ("den", "den2"), ("den2",))
                    tt("dve", dt_, dt_, thr[:, h * NT:(h + 1) * NT], ALU.max, ("den2", "thr"), ("den2",))
                    P.op("dve", lambda e: e.reciprocal(dt_, dt_), ("den2",), ("den2",))
                    tt("dve", ssq[:, NT:2 * NT], dt_, dt_, ALU.mult, ("den2",), ("ssq1",))
                    tt("dve", ssq[:, NT:2 * NT], ssq[:, NT:2 * NT], ssq[:, 0:NT], ALU.mult, ("ssq1", "ssq"), ("ssq1",))
                    act(ssq[:, NT:2 * NT], ssq[:, NT:2 * NT], AF.Ln, ("ssq1",), ("ssq1",), bias=EPS, scale=1.0 / 128)
                    act(ssq[:, 2 * NT:3 * NT], ssq[:, NT:2 * NT], AF.Exp, ("ssq1",), ("ssq2",), scale=-0.5)
                    tt("dve", ssq[:, 2 * NT:3 * NT], ssq[:, 2 * NT:3 * NT], dt_, ALU.mult, ("ssq2", "den2"), ("ssq2",))
                    for t0 in range(0, NT, 4):
                        hv = hb3[:, t0:t0 + 4, :]
                        tt("dve", tmpa[:].rearrange("p (a c) -> p a c", a=4), hv,
                           ssq[:, 2 * NT + t0:2 * NT + t0 + 4].unsqueeze(2).broadcast_to([128, 4, 128]),
                           ALU.mult, ("hbuf", "ssq2"), (("tmp", 0),))
                        tt("dve", tmpa[:].rearrange("p (a c) -> p a c", a=4),
                           tmpa[:].rearrange("p (a c) -> p a c", a=4), W[:, t0:t0 + 4, :], ALU.mult,
                           (("tmp", 0), "W"), (("tmp", 0),))
                        for ti in range(4):
                            tr(ps_t[:, ti, :], tmpa[:, ti * 128:(ti + 1) * 128], ident_b,
                               (("tmp", 0), "cb16"), (("pst", ti),))
                        cp("act", yT[:, 4 + h, t0 * 128:(t0 + 4) * 128],
                           ps_t[:, 0:4, :].rearrange("p a c -> p (a c)"),
                           tuple(("pst", ti) for ti in range(4)), (("yT", 4 + h, t0 // 4),))
            P.barrier()
        else:
            for p in range(4, 8):
                memset("pool", yT[:, p, :], 0.0, (("yT", p, 0),))

        arena_reset()
        d_it, d_flush = phase_D(b) if b + 1 < NSEQ else phase_D(b, nd=5)
        if b + 1 < NSEQ:
            a_it, a_setup = phase_A(b + 1)
            a_setup()
            for i in range(NT + 3):
                a_it(i - 3)
                if i < NT:
                    d_it(i)
        else:
            for i in range(NT):
                d_it(i)
        d_flush()
        P.barrier()

    P.emit()
    es.close()
    return nc


def make_consts(NT):
    p = np.arange(128)[:, None]
    f = np.arange(128)[None, :]
    c = np.zeros((128, 8, 128), np.float32)
    c[:, 0] = (p == f)
    c[:, 1] = (p >= f)
    c[:, 2] = (p < f)
    c[:, 3] = (p <= f)
    c[:, 4] = (p // 64 == f // 64)
    c[:, 5] = 1.0
    c[:, 6] = ((p // NT) == (f // NT)) & ((p % NT) < (f % NT))
    return c


def make_in_maps(inputs, S, NSEQ, n_cores):
    f32 = np.float32
    x = np.asarray(inputs["x"], f32)
    c = np.asarray(inputs["c"], f32)
    w_ada = np.ascontiguousarray(np.asarray(inputs["w_ada"], f32)[0])
    b_ada = np.asarray(inputs["b_ada"], f32)[0]
    norm_gain = np.asarray(inputs["norm_gain"], f32)[0]
    w_in = np.ascontiguousarray(np.asarray(inputs["w_in"], f32)[0])
    b_gates = np.asarray(inputs["b_gates"], f32)[0]
    qg = np.asarray(inputs["q_norm_gain"], f32)[0]
    kg = np.asarray(inputs["k_norm_gain"], f32)[0]
    conv_w = np.asarray(inputs["conv_w"], f32)[0]
    conv_b = np.asarray(inputs["conv_b"], f32)[0]
    mlg = np.asarray(inputs["ml_norm_gain"], f32)[0]
    w_out = np.ascontiguousarray(np.asarray(inputs["w_out"], f32)[0])
    NT = S // 128
    shared = {
        "w_ada": w_ada,
        "b_ada_fm": np.ascontiguousarray(b_ada.reshape(24, 128).T),
        "b_ada_gate": np.ascontiguousarray(b_ada[2048:3072].reshape(1, D)),
        "norm_gain_fm": np.ascontiguousarray(norm_gain.reshape(8, 128).T),
        "w_in": w_in,
        "b_gates_bc": np.ascontiguousarray(np.broadcast_to(b_gates[None, :], (128, 8))),
        "q_gain_fm": np.ascontiguousarray(np.tile(qg, 2).reshape(128, 1)),
        "k_gain_fm": np.ascontiguousarray(np.tile(kg, 2).reshape(128, 1)),
        "conv_w_fm": np.ascontiguousarray(conv_w.reshape(4, 8, 128).transpose(2, 1, 0)),
        "conv_b_fm": np.ascontiguousarray(conv_b.reshape(8, 128).T),
        "ml_gain_bc": np.ascontiguousarray(np.broadcast_to(mlg[None, :], (128, 512))),
        "w_out": w_out,
        "consts": make_consts(NT),
    }
    maps = []
    for i in range(n_cores):
        m = dict(shared)
        m["x"] = np.ascontiguousarray(x[i * NSEQ:(i + 1) * NSEQ])
        cc = c[i * NSEQ:(i + 1) * NSEQ]
        m["cT"] = np.ascontiguousarray(cc.reshape(NSEQ, 8, 128).transpose(2, 1, 0))
        maps.append(m)
    return maps


_NC_CACHE = {}


def kernel(**inputs):
    x = np.asarray(inputs["x"])
    B, S, _ = x.shape
    NSEQ = B // N_CORES
    key = (S, NSEQ)
    if key not in _NC_CACHE:
        _NC_CACHE[key] = build(S, NSEQ)
    nc = _NC_CACHE[key]
    in_maps = make_in_maps(inputs, S, NSEQ, N_CORES)
    res = run_bass_kernel_spmd(nc, in_maps, core_ids=list(range(N_CORES)))
    out = np.concatenate([np.asarray(r["out"], np.float32) for r in res.results], axis=0)
    return out.reshape(B, S, D).astype(np.float32)
```
